# Optimizing a Trainium2 kernel written in Bass

```python
import jax, jax.numpy as jnp
from jax import lax
import numpy as np

D_MODEL = 1024
BATCH = 4
SEQ = 8192
DEPTH = 2

CTX_LEN = 256
GRID_W = 64
EPS = 1e-6
ADA_CHUNKS = 9
D_FF = 2816
FFN_RESIDUAL_WEIGHT = 0.5
D_MIX = D_MODEL
MLA_HEADS = 8
NOPE_DIM = 64
ROPE_DIM = 32
AXIS_DIM = ROPE_DIM // 2
QK_HEAD_DIM = NOPE_DIM + ROPE_DIM
V_HEAD_DIM = 64
Q_LORA = 384
KV_LORA = 256
ATT_WIDTH = MLA_HEADS * V_HEAD_DIM
ROPE_BASE = 10000.0
Q_BLOCK = 128
SG_GROUPS = 4
SG_GROUP_DIM = 64
SG_WIDTH = SG_GROUPS * SG_GROUP_DIM
CHUNK = 128
CONV_WIDTH = 256
CONV_K = 3
OFF_KV = Q_LORA
OFF_KPE = OFF_KV + KV_LORA
OFF_SG = OFF_KPE + ROPE_DIM
OFF_CONV = OFF_SG + 2 * SG_WIDTH
MIX_IN_DIM = OFF_CONV + 3 * CONV_WIDTH

kernel_name = 'hybrid_mla_sgu_shortconv_macaron_dit'


def rms_norm(x, g):
    xf = x.astype(jnp.float32)
    y = xf * lax.rsqrt(jnp.mean(xf * xf, axis=-1, keepdims=True) + EPS)
    return (y * g.astype(jnp.float32)).astype(x.dtype)


def modulate(h, shift, scale):
    return h * (1 + scale) + shift


def ffn_half_step(h, shift, scale, gate, g, w_in, w_out):
    a, b = jnp.split(modulate(rms_norm(h, g), shift, scale) @ w_in, 2, axis=-1)
    return h + FFN_RESIDUAL_WEIGHT * gate * ((jax.nn.silu(a) * b) @ w_out)


def axial_rope_tables(n, dtype):
    rows = n // GRID_W
    row = jnp.repeat(jnp.arange(rows), GRID_W).astype(jnp.float32)
    col = jnp.tile(jnp.arange(GRID_W), rows).astype(jnp.float32)
    inv = 1.0 / (ROPE_BASE ** (jnp.arange(0, AXIS_DIM, 2, dtype=jnp.float32) / AXIS_DIM))
    ang_r = row[:, None] * inv
    ang_c = col[:, None] * inv
    ang = jnp.concatenate([ang_r, ang_r, ang_c, ang_c], axis=-1)
    return jnp.cos(ang).astype(dtype), jnp.sin(ang).astype(dtype)


def rotate_half(t):
    a, b = jnp.split(t, 2, axis=-1)
    return jnp.concatenate([-b, a], axis=-1)


def apply_rope(x, rope):
    cos, sin = rope
    x_nope, x_pe = x[..., :NOPE_DIM], x[..., NOPE_DIM:]
    xr, xc = jnp.split(x_pe, 2, axis=-1)
    rot = jnp.concatenate([rotate_half(xr), rotate_half(xc)], axis=-1)
    x_pe = x_pe * cos[None, :, None, :] + rot * sin[None, :, None, :]
    return jnp.concatenate([x_nope, x_pe], axis=-1)


def mla_queries(q_lat, g_q_lat, w_q_up, g_q_head, rope):
    b, n, _ = q_lat.shape
    q = (rms_norm(q_lat, g_q_lat) @ w_q_up).reshape(b, n, MLA_HEADS, QK_HEAD_DIM)
    q = rms_norm(q, g_q_head)
    return apply_rope(q, rope) if rope is not None else q


def mla_keys_values(kv_lat, k_pe, g_kv_lat, w_kv_up, g_k_head, rope):
    b, n, _ = kv_lat.shape
    kv = (rms_norm(kv_lat, g_kv_lat) @ w_kv_up).reshape(b, n, MLA_HEADS, NOPE_DIM + V_HEAD_DIM)
    k_nope, v = kv[..., :NOPE_DIM], kv[..., NOPE_DIM:]
    k_pe = jnp.broadcast_to(k_pe[:, :, None, :], (b, n, MLA_HEADS, ROPE_DIM))
    k = rms_norm(jnp.concatenate([k_nope, k_pe], axis=-1), g_k_head)
    k = apply_rope(k, rope) if rope is not None else k
    return k, v


def softmax_attend(q, k, v):
    s = jnp.einsum('bqhd,bkhd->bhqk', q, k, preferred_element_type=jnp.float32) * (QK_HEAD_DIM ** -0.5)
    p = jax.nn.softmax(s, axis=-1).astype(v.dtype)
    return jnp.einsum('bhqk,bkhd->bqhd', p, v)


def latent_attention(q, k_lat, v_lat, k_ctx, v_ctx):
    b, n, h, dq = q.shape
    k_all = jnp.concatenate([k_lat, k_ctx], axis=1)
    v_all = jnp.concatenate([v_lat, v_ctx], axis=1)
    qb = q.reshape(b, n // Q_BLOCK, Q_BLOCK, h, dq).transpose(1, 0, 2, 3, 4)
    o = lax.map(lambda qblk: softmax_attend(qblk, k_all, v_all), qb)
    return o.transpose(1, 0, 2, 3, 4).reshape(b, n, h * V_HEAD_DIM)


def spatial_gating(sg_in, g_sgu, w_spatial, b_spatial):
    b, n, _ = sg_in.shape
    u, v = jnp.split(jax.nn.gelu(sg_in), 2, axis=-1)
    v = rms_norm(v.reshape(b, n, SG_GROUPS, SG_GROUP_DIM), g_sgu)
    v = v.reshape(b, n // CHUNK, CHUNK, SG_GROUPS, SG_GROUP_DIM)
    vs = jnp.einsum('gpq,bnqgc->bnpgc', w_spatial, v) + b_spatial.T[:, :, None]
    return u * vs.reshape(b, n, SG_WIDTH)


def short_conv(cv_in, w_conv):
    b_gate, c_gate, xin = jnp.split(cv_in, 3, axis=-1)
    z = c_gate * xin
    y = lax.conv_general_dilated(z, w_conv[:, None, :], window_strides=(1,), padding=[(1, 1)],
                                 dimension_numbers=('NWC', 'WIO', 'NWC'), feature_group_count=CONV_WIDTH)
    return b_gate * y


def merge_groups(attn, sg, cv, g_out, w_mix_out):
    y = jnp.concatenate([rms_norm(attn, g_out[:ATT_WIDTH]),
                         rms_norm(sg, g_out[ATT_WIDTH:ATT_WIDTH + SG_WIDTH]),
                         rms_norm(cv, g_out[ATT_WIDTH + SG_WIDTH:])], axis=-1)
    return y @ w_mix_out


def setup_inputs(seed: int = 0) -> dict:
    key = jax.random.key(seed)
    ks = iter(jax.random.split(key, 40))

    def nrm(shape, scale):
        return jax.random.normal(next(ks), shape, jnp.float32) * scale

    def gain(shape):
        return 1.0 + nrm(shape, 0.02)

    L, D = DEPTH, D_MODEL
    return {
        'x': nrm((BATCH, SEQ, D), 1.0),
        'c': nrm((BATCH, D), 1.0),
        'ctx': nrm((BATCH, CTX_LEN, D), 1.0),
        'c_ctx': nrm((D,), 1.0),
        'w_ada': nrm((L, D, ADA_CHUNKS * D), 0.5 * D ** -0.5),
        'b_ada': nrm((L, ADA_CHUNKS * D), 0.02),
        'g_ffn1': gain((L, D)),
        'w_ffn1_in': nrm((L, D, 2 * D_FF), D ** -0.5),
        'w_ffn1_out': nrm((L, D_FF, D), D_FF ** -0.5),
        'g_mix': gain((L, D)),
        'w_mix_in': nrm((L, D, MIX_IN_DIM), D ** -0.5),
        'g_q_lat': gain((L, Q_LORA)),
        'w_q_up': nrm((L, Q_LORA, MLA_HEADS * QK_HEAD_DIM), Q_LORA ** -0.5),
        'g_kv_lat': gain((L, KV_LORA)),
        'w_kv_up': nrm((L, KV_LORA, MLA_HEADS * (NOPE_DIM + V_HEAD_DIM)), KV_LORA ** -0.5),
        'g_q_head': gain((L, QK_HEAD_DIM)),
        'g_k_head': gain((L, QK_HEAD_DIM)),
        'g_sgu': gain((L, SG_GROUPS, SG_GROUP_DIM)),
        'w_spatial': nrm((L, SG_GROUPS, CHUNK, CHUNK), CHUNK ** -0.5),
        'b_spatial': 1.0 + nrm((L, SG_GROUPS, CHUNK), 0.02),
        'w_conv': nrm((L, CONV_K, CONV_WIDTH), CONV_K ** -0.5),
        'g_out': gain((L, D_MIX)),
        'w_mix_out': nrm((L, D_MIX, D), D_MIX ** -0.5),
        'g_ffn2': gain((L, D)),
        'w_ffn2_in': nrm((L, D, 2 * D_FF), D ** -0.5),
        'w_ffn2_out': nrm((L, D_FF, D), D_FF ** -0.5),
    }


def reference(x, c, ctx, c_ctx, w_ada, b_ada, g_ffn1, w_ffn1_in, w_ffn1_out, g_mix, w_mix_in,
              g_q_lat, w_q_up, g_kv_lat, w_kv_up, g_q_head, g_k_head, g_sgu, w_spatial, b_spatial,
              w_conv, g_out, w_mix_out, g_ffn2, w_ffn2_in, w_ffn2_out):
    rope = axial_rope_tables(x.shape[1], x.dtype)
    h, hc = x, ctx
    sc, scc = jax.nn.silu(c), jax.nn.silu(c_ctx)
    splits = [OFF_KV, OFF_KPE, OFF_SG, OFF_CONV]
    for l in range(DEPTH):
        last = l == DEPTH - 1
        mod_l = jnp.split((sc @ w_ada[l] + b_ada[l])[:, None, :], ADA_CHUNKS, axis=-1)
        mod_c = jnp.split((scc @ w_ada[l] + b_ada[l])[None, None, :], ADA_CHUNKS, axis=-1)

        h = ffn_half_step(h, mod_l[0], mod_l[1], mod_l[2], g_ffn1[l], w_ffn1_in[l], w_ffn1_out[l])
        hc = ffn_half_step(hc, mod_c[0], mod_c[1], mod_c[2], g_ffn1[l], w_ffn1_in[l], w_ffn1_out[l])

        hn = modulate(rms_norm(h, g_mix[l]), mod_l[3], mod_l[4])
        hnc = modulate(rms_norm(hc, g_mix[l]), mod_c[3], mod_c[4])
        q_lat, kv_lat, k_pe, sg_in, cv_in = jnp.split(hn @ w_mix_in[l], splits, axis=-1)
        if last:
            kv_lat_c, k_pe_c = jnp.split(hnc @ w_mix_in[l][:, OFF_KV:OFF_SG], [KV_LORA], axis=-1)
        else:
            q_lat_c, kv_lat_c, k_pe_c, sg_in_c, cv_in_c = jnp.split(hnc @ w_mix_in[l], splits, axis=-1)
        q = mla_queries(q_lat, g_q_lat[l], w_q_up[l], g_q_head[l], rope)
        k, v = mla_keys_values(kv_lat, k_pe, g_kv_lat[l], w_kv_up[l], g_k_head[l], rope)
        kc, vc = mla_keys_values(kv_lat_c, k_pe_c, g_kv_lat[l], w_kv_up[l], g_k_head[l], None)
        attn = latent_attention(q, k, v, kc, vc)
        sg = spatial_gating(sg_in, g_sgu[l], w_spatial[l], b_spatial[l])
        cv = short_conv(cv_in, w_conv[l])
        h = h + mod_l[5] * merge_groups(attn, sg, cv, g_out[l], w_mix_out[l])

        h = ffn_half_step(h, mod_l[6], mod_l[7], mod_l[8], g_ffn2[l], w_ffn2_in[l], w_ffn2_out[l])

        if not last:
            qc = mla_queries(q_lat_c, g_q_lat[l], w_q_up[l], g_q_head[l], None)
            b_sz = hc.shape[0]
            attn_c = softmax_attend(qc, kc, vc).reshape(b_sz, hc.shape[1], ATT_WIDTH)
            sg_c = spatial_gating(sg_in_c, g_sgu[l], w_spatial[l], b_spatial[l])
            cv_c = short_conv(cv_in_c, w_conv[l])
            hc = hc + mod_c[5] * merge_groups(attn_c, sg_c, cv_c, g_out[l], w_mix_out[l])
            hc = ffn_half_step(hc, mod_c[6], mod_c[7], mod_c[8], g_ffn2[l], w_ffn2_in[l], w_ffn2_out[l])
    return h
```

```python
import numpy as np
from contextlib import ExitStack
import concourse.bass as bass
import concourse.mybir as mybir
from concourse.bass_utils import run_bass_kernel_spmd

F32 = mybir.dt.float32
BF16 = mybir.dt.bfloat16
AF = mybir.ActivationFunctionType
ALU = mybir.AluOpType
AX = mybir.AxisListType

D = 1024
DFF = 2816
NJ = 22
SEQ_C = 4096
CTX = 256
NTOK = SEQ_C + CTX
TB = 512
EPS = 1e-6
DEPTH = 2
NH = 8
QK = 96
MIXC = 1952
JGROUPS = [(0, 6), (6, 12), (12, 17), (17, 22)]
V_GF1, V_GMIX, V_GF2, V_GQL, V_GKV, V_GQH, V_GQHP, V_GKH, V_GKHP, V_GOA, V_GOS, V_GOC, V_WCV = \
    0, 8, 16, 24, 27, 29, 30, 31, 32, 33, 41, 45, 47
NVEC = 53


class Sem:
    _n = 0

    def __init__(self, h):
        self.h = h
        Sem._n += 1
        self.key = Sem._n
        self.count = 0


class Buf:
    __slots__ = ("name", "w", "r", "dsem", "excl")

    def __init__(self, name=""):
        self.name = name
        self.w = None
        self.r = {}
        self.dsem = None
        self.excl = False


class Eng:
    def __init__(self, name, eng):
        self.name = name
        self.eng = eng
        self.sem = None
        self.seen = {}


class K:
    def __init__(self, nc, es):
        self.nc = nc
        self.es = es
        self.engs = {"pe": Eng("pe", nc.tensor), "act": Eng("act", nc.scalar), "dve": Eng("dve", nc.vector),
                     "pool": Eng("pool", nc.gpsimd), "sp": Eng("sp", nc.sync)}
        self.free_dsems = {}
        self.live_dsems = []
        self.bufs = []
        self.nsem = 0
        self.new_epoch()

    def new_sem(self, name):
        self.nsem += 1
        return Sem(self.es.enter_context(self.nc.semaphore(f"{name}_{self.nsem}")))

    def new_epoch(self):
        for e in self.engs.values():
            e.sem = self.new_sem("e" + e.name)
            e.seen = {}

    def buf(self, name=""):
        b = Buf(name)
        self.bufs.append(b)
        return b

    def wait(self, e, stamp):
        sem, val = stamp
        if e.name == "pe" and sem is e.sem:
            return
        if e.seen.get(sem.key, 0) < val:
            e.eng.wait_ge(sem.h, val)
            e.seen[sem.key] = val

    def deps(self, e, reads, writes):
        for b in reads:
            if b.w is not None:
                self.wait(e, b.w)
            if b.excl:
                for st in b.r.values():
                    if st[0] is not e.sem:
                        self.wait(e, st)
        for b in writes:
            if b.w is not None and b.w[0] is not e.sem:
                self.wait(e, b.w)
            for st in b.r.values():
                self.wait(e, st)

    def mark(self, stamp, reads, writes):
        sk = stamp[0].key
        for b in reads:
            old = b.r.get(sk)
            if old is None or old[1] < stamp[1]:
                b.r[sk] = stamp
        for b in writes:
            b.w = stamp
            b.r = {}

    def op(self, en, fn, reads=(), writes=(), signal=True):
        e = self.engs[en]
        self.deps(e, reads, writes)
        inst = fn(e.eng)
        if signal:
            e.sem.count += 1
            inst.then_inc(e.sem.h, 1)
            stamp = (e.sem, e.sem.count)
        else:
            stamp = (e.sem, e.sem.count + 1)
        self.mark(stamp, reads, writes)
        return inst

    def dsem_for(self, b, qn):
        if b.dsem is None:
            fp = self.free_dsems.setdefault(qn, [])
            b.dsem = fp.pop() if fp else self.new_sem("d" + qn)
            b.dsem.q = qn
            self.live_dsems.append(b)
        assert b.dsem.q == qn, (b.name, b.dsem.q, qn)
        return b.dsem

    def dma(self, qn, out, in_, reads=(), writes=(), slow=False):
        e = self.engs[qn]
        self.deps(e, reads, writes)
        if slow:
            inst = e.eng.dma_start(out=out, in_=in_, allow_slow_non_contiguous=True)
        else:
            inst = e.eng.dma_start(out=out, in_=in_)
        s = self.dsem_for(writes[0], qn)
        s.count += 16
        inst.then_inc(s.h, 16)
        self.mark((s, s.count), reads, writes)
        return inst

    def barrier(self):
        es = list(self.engs.values())
        stamps = [(e.sem, e.sem.count) for e in es if e.sem.count > 0]
        for b in self.live_dsems:
            stamps.append((b.dsem, b.dsem.count))
        for e in es:
            for st in stamps:
                if st[0] is not e.sem:
                    self.wait(e, st)
        for b in self.live_dsems:
            self.free_dsems[b.dsem.q].append(b.dsem)
            b.dsem = None
        self.live_dsems = []
        for b in self.bufs:
            b.w = None
            b.r = {}
        self.bufs = [b for b in self.bufs if b.name.startswith("P:")]

    def mm(self, out, lhsT, rhs, start, stop, R, W, sig=None):
        return self.op("pe", lambda e: e.matmul(out, lhsT, rhs, start=start, stop=stop), R, W,
                       signal=(stop if sig is None else sig))

    def act(self, out, in_, func, R, W, bias=0.0, scale=1.0):
        return self.op("act", lambda e: e.activation(out=out, in_=in_, func=func, bias=bias, scale=scale), R, W)

    def tt(self, out, in0, in1, op, R, W, en="dve"):
        return self.op(en, lambda e: e.tensor_tensor(out=out, in0=in0, in1=in1, op=op), R, W)

    def stt(self, out, in0, scalar, in1, op0, op1, R, W, en="dve"):
        return self.op(en, lambda e: e.scalar_tensor_tensor(out=out, in0=in0, scalar=scalar, in1=in1,
                                                            op0=op0, op1=op1), R, W)

    def ts(self, out, in0, s1, s2, op0, op1, R, W, en="dve"):
        if s2 is None:
            return self.op(en, lambda e: e.tensor_scalar(out=out, in0=in0, scalar1=s1, scalar2=None, op0=op0), R, W)
        return self.op(en, lambda e: e.tensor_scalar(out=out, in0=in0, scalar1=s1, scalar2=s2, op0=op0, op1=op1),
                       R, W)

    def copy(self, out, in_, R, W, en="dve"):
        return self.op(en, lambda e: e.tensor_copy(out=out, in_=in_), R, W)

    def recip(self, out, in_, R, W):
        return self.op("dve", lambda e: e.reciprocal(out=out, in_=in_), R, W)

    def memset(self, ap, val, W, en="dve"):
        return self.op(en, lambda e: e.memset(ap, val), (), W)


class Ring:
    def __init__(self, k, t, n, name):
        self.t = t
        self.n = n
        self.bufs = [k.buf(f"{name}{i}") for i in range(n)]
        self.i = -1

    def next(self):
        self.i += 1
        j = self.i % self.n
        return self.t[:, j], self.bufs[j]


def build_program(debug=False, stop_after=None, ncores=8):
    nc = bass.Bass("TRN2", target_bir_lowering=False)
    es = ExitStack()

    def din(name, shape, dt=F32):
        return nc.dram_tensor(name, list(shape), dt, kind="ExternalInput").ap()

    dbg_kind = "ExternalOutput" if debug else None

    def dscr(name, shape, dt=F32, cc=False):
        if debug and not cc:
            return nc.dram_tensor(name, list(shape), dt, kind="ExternalOutput").ap()
        return nc.dram_tensor(name, list(shape), dt).ap()

    x = din("x", [SEQ_C, D])
    ctxin = din("ctx", [CTX, D])
    cc = din("cc", [128, 8, 2])
    rope = din("rope", [2, QK, NTOK])
    hmask = din("hmask", [128, 2])
    ident_d = din("ident", [128, 128])
    sel_d = din("sel", [32, 2 * QK])
    w_ada = din("w_ada", [DEPTH, D, 9 * D])
    b_ada_t = din("b_ada_t", [DEPTH, 128, 72])
    w_f_in = [din("w_ffn1_in", [DEPTH, D, 2 * DFF]), din("w_ffn2_in", [DEPTH, D, 2 * DFF])]
    w_f_out = [din("w_ffn1_out", [DEPTH, DFF, D]), din("w_ffn2_out", [DEPTH, DFF, D])]
    w_mi = din("w_mix_in", [DEPTH, D, MIXC])
    w_qu = din("w_qu", [DEPTH, 384, 2 * NH * QK])
    w_kvk = din("w_kvk", [DEPTH, 256, NH * QK])
    w_kvv = din("w_kvv", [DEPTH, 256, NH * 64])
    w_spT = din("w_spT", [DEPTH, 4, 128, 128])
    b_sp = din("b_sp", [DEPTH, 1, 512])
    w_mo = din("w_mix_out", [DEPTH, D, D])
    vecs_d = din("vecs", [DEPTH, 128, NVEC])
    gsgu_d = din("gsgu_b", [DEPTH, 128, 256])
    grow_d = din("grow", [DEPTH, 1, 2 * QK])
    out = nc.dram_tensor("out", [SEQ_C, D], F32, kind="ExternalOutput").ap()

    hS = dscr("hS", [8, 128, 8, TB])
    hC = dscr("hC", [128, 8, CTX])
    qS = dscr("qS", [NH, QK, NTOK], BF16)
    kI = [dscr(f"kI{c}", [2 * QK, NTOK], BF16, cc=True) for c in range(4)]
    kO = [dscr(f"kO{c}", [4 * QK, NTOK], BF16, cc=True) for c in range(4)]
    vI = [dscr(f"vI{c}", [2 * 128, 34 * 64], BF16, cc=True) for c in range(4)]
    vO = [dscr(f"vO{c}", [4 * 128, 34 * 64], BF16, cc=True) for c in range(4)]
    aS = dscr("aS", [NH, 64, NTOK])
    sgS = dscr("sgS", [4, 64, NTOK], BF16)
    zS = dscr("zS", [2, 128, SEQ_C + 2])
    zC = dscr("zC", [2, 128, CTX + 2])
    bgS = dscr("bgS", [2, 128, NTOK])
    zhI = dscr("zhI", [2, 256], cc=True)
    zhO = dscr("zhO", [4, 256], cc=True)

    k = K(nc, es)
    cc_sem = k.new_sem("cc")
    cc_count = [0]

    ucnt = [0]

    def sbt(name, shape, dt=F32):
        ucnt[0] += 1
        return nc.sbuf_tensor(f"{name}_u{ucnt[0]}", list(shape), dt)

    def sb(name, shape, dt=F32):
        return es.enter_context(sbt(name, shape, dt))

    ident = sb("ident", [128, 128]);           B_ident = k.buf("P:ident")
    ones_b = sb("ones_b", [128, 128], BF16);   B_ones = k.buf("P:ones")
    ones_f = sb("ones_f", [1, 128]);           B_onesf = k.buf("P:onesf")
    sel = sb("sel", [32, 2 * QK], BF16);       B_sel = k.buf("P:sel")
    scb = sb("scb", [128, 8, 2], BF16);        B_scb = k.buf("P:scb")
    modv_l = [sb("modv", [128, 2, 72]) for _ in range(DEPTH)];   B_modv_l = [k.buf("P:modv") for _ in range(DEPTH)]
    der_l = [sb("der", [128, 2, 72]) for _ in range(DEPTH)];     B_der_l = [k.buf("P:der") for _ in range(DEPTH)]
    vecs_l = [sb("vecs", [128, NVEC]) for _ in range(DEPTH)];    B_vecs_l = [k.buf("P:vecs") for _ in range(DEPTH)]
    modv, der, vecs = modv_l[0], der_l[0], vecs_l[0]
    B_modv, B_der, B_vecs = B_modv_l[0], B_der_l[0], B_vecs_l[0]

    def set_layer(l):
        nonlocal modv, der, vecs, B_modv, B_der, B_vecs
        modv, der, vecs = modv_l[l], der_l[l], vecs_l[l]
        B_modv, B_der, B_vecs = B_modv_l[l], B_der_l[l], B_vecs_l[l]
    hm = sb("hm", [128, 2]);                   B_hm = k.buf("P:hm")
    nshift = sb("nshift", [128, 1]);           B_nshift = k.buf("P:nshift")
    zero_t = sb("zero_t", [128, 2]);           B_zero = k.buf("P:zero")
    zh = sb("zh", [128, 2, 2]);               B_zh = k.buf("P:zh")
    xb = {"k": [None] * 4, "v": [None] * 4}
    psum = es.enter_context(nc.psum_tensor("psum", [128, 8, TB], F32))
    PB = [k.buf(f"P:ps{i}") for i in range(8)]
    for b_ in PB:
        b_.excl = True
    ps_i = [-1]

    ps_reserved = set()

    def ps_next():
        while True:
            ps_i[0] += 1
            j = ps_i[0] % 8
            if j not in ps_reserved:
                return psum[:, j], PB[j]

    blocks = [(hS[i], TB, i * TB, 0) for i in range(8)] + [(hC, CTX, SEQ_C, 1)]

    k.dma("sp", ident[:], ident_d, (), [B_ident])
    k.dma("pool", sel[:], sel_d, (), [B_sel])
    k.dma("sp", hm[:], hmask, (), [B_hm])
    k.memset(ones_b[:], 1.0, [B_ones])
    k.memset(ones_f[:], 1.0, [B_onesf])
    k.memset(zero_t[:], 0.0, [B_zero])
    for c in range(2):
        k.dma("sp", zC[c, :, 0:1], zero_t[:, 0:1], [B_zero], [k.buf("zc0")], slow=True)
        k.dma("sp", zC[c, :, CTX + 1:CTX + 2], zero_t[:, 0:1], [B_zero], [k.buf("zc1")], slow=True)
    with sbt("cc_t", [128, 8, 2], F32) as cc_t:
        B_cc = k.buf("cc")
        k.dma("sp", cc_t[:], cc, (), [B_cc])
        k.act(scb[:], cc_t[:], AF.Silu, [B_cc], [B_scb])
        k.barrier()

    def rstd_of(ssum_ap, P, T, inv_n, R, rt, rs):
        rt_ap, rt_b = rt.next()
        rs_ap, rs_b = rs.next()
        k.act(rt_ap[0:P, 0:T], ssum_ap, AF.Ln, R, [rt_b], bias=eps_col[0:P, :], scale=inv_n)
        k.act(rs_ap[0:P, 0:T], rt_ap[0:P, 0:T], AF.Exp, [rt_b], [rs_b], scale=-0.5)
        return rs_ap, rs_b

    def phase_tin():
        with ExitStack() as ph:
            xt = Ring(k, ph.enter_context(sbt("xt", [128, 2, 4, D], F32)), 2, "xt")
            ht = Ring(k, ph.enter_context(sbt("ht", [128, 2, 8, TB], F32)), 2, "ht")
            n = 0
            ada = AdaJob(0, ph)
            ada.dma(0)
            for bi_, (hd, T, col0, stream) in enumerate(blocks):
                src = x[col0:col0 + T, :] if stream == 0 else ctxin
                nt = T // 128
                xa, xb = xt.next()
                k.dma("sp", xa[:, 0:nt, :], src.rearrange("(j p) f -> p j f", p=128), (), [xb])
                ha, hb = ht.next()
                for c in range(8):
                    pa, pb = ps_next()
                    for j in range(nt):
                        k.op("pe", lambda e: e.transpose(pa[:, j * 128:(j + 1) * 128], xa[:, j, c * 128:(c + 1) * 128],
                                                         ident[:]), [xb, B_ident], [pb], signal=(j == nt - 1))
                    if n % 2 == 0:
                        k.copy(ha[:, c, 0:T], pa[:, 0:T], [pb], [hb])
                    else:
                        k.act(ha[:, c, 0:T], pa[:, 0:T], AF.Copy, [pb], [hb])
                n += 1
                hB = k.buf("hS")
                k.dma("sp", hd, ha[:, :, 0:T], [hb], [hB])
                ada.step(bi_)
            ada.finish()
            k.barrier()

    def phase_tout():
        with ExitStack() as ph:
            ht = Ring(k, ph.enter_context(sbt("ht", [128, 2, 8, TB], F32)), 2, "ht")
            ot = Ring(k, ph.enter_context(sbt("ot", [128, 2, 4, D], F32)), 2, "ot")
            oB = k.buf("out")
            n = 0
            for (hd, T, col0, stream) in blocks[:8]:
                ha, hb = ht.next()
                k.dma("sp", ha[:], hd, (), [hb])
                oa, ob = ot.next()
                for j in range(4):
                    for c2 in range(2):
                        pa, pb = ps_next()
                        for cc_ in range(4):
                            c = c2 * 4 + cc_
                            k.op("pe", lambda e: e.transpose(pa[:, cc_ * 128:(cc_ + 1) * 128],
                                                             ha[:, c, j * 128:(j + 1) * 128], ident[:]),
                                 [hb, B_ident], [pb], signal=(cc_ == 3))
                        if n % 2 == 0:
                            k.copy(oa[:, j, c2 * 512:(c2 + 1) * 512], pa[:], [pb], [ob])
                        else:
                            k.act(oa[:, j, c2 * 512:(c2 + 1) * 512], pa[:], AF.Copy, [pb], [ob])
                n += 1
                k.dma("sp", out[col0:col0 + T, :].rearrange("(j p) f -> p j f", p=128), oa[:], [ob], [oB])
            k.barrier()

    class AdaJob:
        def __init__(self, l, ph):
            self.l = l
            self.wa = Ring(k, ph.enter_context(sbt("wa", [128, 2, 8, D], BF16)), 2, "wa")
            self.bt = ph.enter_context(sbt("bt", [128, 72], F32))
            self.B_bt = k.buf("bt")
            k.dma("sp", self.bt[:], b_ada_t[l], (), [self.B_bt])
            k.dma("sp", vecs_l[l][:], vecs_d[l], (), [B_vecs_l[l]])
            ps_i[0] += 1
            while ps_i[0] % 8 in ps_reserved:
                ps_i[0] += 1
            self.j = ps_i[0] % 8
            ps_reserved.add(self.j)
            self.pa, self.pb = psum[:, self.j], PB[self.j]
            self.wv = w_ada[l].rearrange("(k p) n -> p k n", p=128)
            self.slots = {}

        def dma(self, ch):
            if ch < 9 and ch not in self.slots:
                wa_ap, wa_b = self.wa.next()
                k.dma("pool", wa_ap, self.wv[:, :, ch * D:(ch + 1) * D], (), [wa_b])
                self.slots[ch] = (wa_ap, wa_b)

        def step(self, ch):
            if ch >= 9:
                return
            self.dma(ch)
            wa_ap, wa_b = self.slots[ch]
            for fo in range(8):
                j = ch * 8 + fo
                for kk in range(8):
                    k.mm(self.pa[:, 2 * j:2 * j + 2], wa_ap[:, kk, fo * 128:(fo + 1) * 128], scb[:, kk, :],
                         kk == 0, kk == 7, [wa_b, B_scb], [self.pb])
            self.dma(ch + 1)

        def finish(self):
            l = self.l
            mv, dr, vc = modv_l[l], der_l[l], vecs_l[l]
            Bm, Bd, Bv = B_modv_l[l], B_der_l[l], B_vecs_l[l]
            pa, pb = self.pa, self.pb
            for s_ in range(2):
                k.tt(mv[:, s_, :], pa[:, 0:144].rearrange("p (j s) -> p j s", s=2)[:, :, s_], self.bt[:], ALU.add,
                     [pb, self.B_bt], [Bm])
            for s_ in range(2):
                for (ch_scale, vcol, ch_shift, ch_gate, gmul) in ((1, V_GF1, 0, 2, 0.5), (4, V_GMIX, 3, 5, 1.0),
                                                                  (7, V_GF2, 6, 8, 0.5)):
                    k.stt(dr[:, s_, ch_scale * 8:ch_scale * 8 + 8], mv[:, s_, ch_scale * 8:ch_scale * 8 + 8], 1.0,
                          vc[:, vcol:vcol + 8], ALU.add, ALU.mult, [Bm, Bv], [Bd])
                    k.copy(dr[:, s_, ch_shift * 8:ch_shift * 8 + 8], mv[:, s_, ch_shift * 8:ch_shift * 8 + 8], [Bm], [Bd])
                    k.ts(dr[:, s_, ch_gate * 8:ch_gate * 8 + 8], mv[:, s_, ch_gate * 8:ch_gate * 8 + 8], gmul, None,
                         ALU.mult, None, [Bm], [Bd])
            ps_reserved.discard(self.j)

    def norm_modulate(ph_rings, ha, hb, T, stream, ch_scale, ch_shift, xn_ap, xn_b):
        sq, rt, rs, tmp = ph_rings
        pa, pb = ps_next()
        for c in range(8):
            sq_ap, sq_b = sq.next()
            k.act(sq_ap[:, 0:T], ha[:, c, 0:T], AF.Square, [hb], [sq_b])
            k.mm(pa[:, 0:T], ones_b[:], sq_ap[:, 0:T], c == 0, c == 7, [sq_b, B_ones], [pb], sig=True)
        rs_ap, rs_b = rstd_of(pa[:, 0:T], 128, T, 1.0 / D, [pb], rt, rs)
        for c in range(8):
            t_ap, t_b = tmp.next()
            k.stt(t_ap[:, 0:T], ha[:, c, 0:T], der[:, stream, ch_scale * 8 + c:ch_scale * 8 + c + 1], rs_ap[:, 0:T],
                  ALU.mult, ALU.mult, [hb, rs_b, B_der], [t_b])
            k.act(xn_ap[:, c, 0:T], t_ap[:, 0:T], AF.Identity, [t_b, B_der], [xn_b],
                  bias=der[:, stream, ch_shift * 8 + c:ch_shift * 8 + c + 1])

    def phase_ffn(l, which, streams):
        ch_shift, ch_scale, ch_gate = (0, 1, 2) if which == 0 else (6, 7, 8)
        with ExitStack() as ph:
            win = ph.enter_context(sbt("win", [128, 8, 2 * DFF], BF16))
            wout = ph.enter_context(sbt("wout", [128, NJ, D], BF16))
            ht = Ring(k, ph.enter_context(sbt("ht", [128, 2, 8, TB], F32)), 2, "ht")
            sq = Ring(k, ph.enter_context(sbt("sq", [128, 2, TB], BF16)), 2, "sq")
            rt = Ring(k, ph.enter_context(sbt("rt", [128, 1, TB], F32)), 1, "rt")
            rs = Ring(k, ph.enter_context(sbt("rs", [128, 1, TB], F32)), 1, "rs")
            tmp = Ring(k, ph.enter_context(sbt("tmp", [128, 2, TB], F32)), 2, "tmp")
            sa = Ring(k, ph.enter_context(sbt("sa", [128, 2, TB], F32)), 2, "sa")
            xn = Ring(k, ph.enter_context(sbt("xn", [128, 2, 8, TB], BF16)), 2, "xn")
            g = ph.enter_context(sbt("g", [128, 6, TB], BF16)); B_g = k.buf("g")
            wiv = w_f_in[which][l].rearrange("(k p) n -> p k n", p=128)
            wov = w_f_out[which][l].rearrange("(j p) n -> p j n", p=128)
            B_win, B_wout = [], []
            for (j0, j1) in JGROUPS:
                bw = k.buf("win"); bo = k.buf("wout")
                k.dma("pool", win[:, :, j0 * 128:j1 * 128], wiv[:, :, j0 * 128:j1 * 128], (), [bw])
                k.dma("pool", win[:, :, DFF + j0 * 128:DFF + j1 * 128], wiv[:, :, DFF + j0 * 128:DFF + j1 * 128], (), [bw])
                k.dma("pool", wout[:, j0:j1, :], wov[:, j0:j1, :], (), [bo])
                B_win.append(bw); B_wout.append(bo)
            blks = [b for b in blocks if b[3] in streams]
            loaded = {}

            def load(i):
                if i < len(blks) and i not in loaded:
                    hd, T, col0, stream = blks[i]
                    ha, hb = ht.next()
                    k.dma("sp", ha[:, :, 0:T], hd, (), [hb])
                    loaded[i] = (ha, hb)
            load(0)
            xns = {}

            def prep(i):
                if i < len(blks):
                    hd_, T_, col0_, stream_ = blks[i]
                    ha_, hb_ = loaded[i]
                    xa_, xb_ = xn.next()
                    norm_modulate((sq, rt, rs, tmp), ha_, hb_, T_, stream_, ch_scale, ch_shift, xa_, xb_)
                    xns[i] = (xa_, xb_)
            prep(0)
            for bi, (hd, T, col0, stream) in enumerate(blks):
                ha, hb = loaded[bi]
                load(bi + 1)
                xn_ap, B_xn = xns[bi]
                for gi, (j0, j1) in enumerate(JGROUPS):
                    if gi == 2:
                        prep(bi + 1)
                    for j in range(j0, j1):
                        pa, pab = ps_next()
                        pb_, pbb = ps_next()
                        for kk in range(8):
                            k.mm(pa[:, 0:T], win[:, kk, j * 128:(j + 1) * 128], xn_ap[:, kk, 0:T], kk == 0, kk == 7,
                                 [B_win[gi], B_xn], [pab])
                        for kk in range(8):
                            k.mm(pb_[:, 0:T], win[:, kk, DFF + j * 128:DFF + (j + 1) * 128], xn_ap[:, kk, 0:T], kk == 0,
                                 kk == 7, [B_win[gi], B_xn], [pbb])
                        sa_ap, sa_b = sa.next()
                        k.act(sa_ap[:, 0:T], pa[:, 0:T], AF.Silu, [pab], [sa_b])
                        k.tt(g[:, j - j0, 0:T], sa_ap[:, 0:T], pb_[:, 0:T], ALU.mult, [sa_b, pbb], [B_g])
                    for c in range(8):
                        po, pob = ps_next()
                        for j in range(j0, j1):
                            k.mm(po[:, 0:T], wout[:, j, c * 128:(c + 1) * 128], g[:, j - j0, 0:T], j == j0, j == j1 - 1,
                                 [B_wout[gi], B_g], [pob])
                        k.stt(ha[:, c, 0:T], po[:, 0:T], der[:, stream, ch_gate * 8 + c:ch_gate * 8 + c + 1],
                              ha[:, c, 0:T], ALU.mult, ALU.add, [pob, B_der, hb], [hb])
                k.dma("sp", hd, ha[:, :, 0:T], [hb], [k.buf("hS")])
            k.barrier()

    def phase_mix(l, last):
        with ExitStack() as ph:
            wmi = ph.enter_context(sbt("wmi", [128, 8, MIXC], BF16)); B_wmi = k.buf("wmi")
            wqu = ph.enter_context(sbt("wqu", [128, 3, 2 * NH * QK], BF16)); B_wqu = k.buf("wqu")
            wkk = ph.enter_context(sbt("wkk", [128, 2, NH * QK], BF16)); B_wkk = k.buf("wkk")
            wkv = ph.enter_context(sbt("wkv", [128, 2, NH * 64], BF16)); B_wkv = k.buf("wkv")
            wsp = ph.enter_context(sbt("wsp", [128, 4, 128], BF16)); B_wsp = k.buf("wsp")
            bsp = ph.enter_context(sbt("bsp", [1, 512], F32)); B_bsp = k.buf("bsp")
            gsg = ph.enter_context(sbt("gsg", [128, 256], F32)); B_gsg = k.buf("gsg")
            grow = ph.enter_context(sbt("grow", [1, 2 * QK], F32)); B_grow = k.buf("grow")
            gmax = ph.enter_context(sbt("gmax", [1, 4], F32)); B_gmax = k.buf("gmax")
            k.dma("pool", wmi[:], w_mi[l].rearrange("(k p) n -> p k n", p=128), (), [B_wmi])
            k.dma("pool", wqu[:], w_qu[l].rearrange("(k p) n -> p k n", p=128), (), [B_wqu])
            k.dma("pool", wkk[:], w_kvk[l].rearrange("(k p) n -> p k n", p=128), (), [B_wkk])
            k.dma("pool", wkv[:], w_kvv[l].rearrange("(k p) n -> p k n", p=128), (), [B_wkv])
            k.dma("pool", wsp[:], w_spT[l].rearrange("g q p -> q g p"), (), [B_wsp])
            k.dma("sp", bsp[:], b_sp[l], (), [B_bsp])
            k.dma("sp", gsg[:], gsgu_d[l], (), [B_gsg])
            k.dma("sp", grow[:], grow_d[l], (), [B_grow])
            k.op("dve", lambda e: e.tensor_reduce(out=gmax[:, 0:1], in_=grow[:, 0:QK], axis=AX.X, op=ALU.max,
                                                  apply_absolute_value=True), [B_grow], [B_gmax])
            k.op("dve", lambda e: e.tensor_reduce(out=gmax[:, 1:2], in_=grow[:, QK:2 * QK], axis=AX.X, op=ALU.max,
                                                  apply_absolute_value=True), [B_grow], [B_gmax])
            k.stt(gmax[:, 2:3], gmax[:, 0:1], -float(np.sqrt(QK)), gmax[:, 1:2], ALU.mult, ALU.mult,
                  [B_gmax], [B_gmax])
            pa, pb = ps_next()
            k.mm(pa[:, 0:1], ones_f[:], gmax[:, 2:3], True, True, [B_onesf, B_gmax], [pb])
            k.copy(nshift[:], pa[:, 0:1], [pb], [B_nshift])

            ht = Ring(k, ph.enter_context(sbt("ht", [128, 2, 8, TB], F32)), 2, "ht")
            rp = Ring(k, ph.enter_context(sbt("rp", [QK, 2, 2, TB], F32)), 2, "rp")
            sq = Ring(k, ph.enter_context(sbt("sq", [128, 3, TB], BF16)), 3, "sq")
            rt = Ring(k, ph.enter_context(sbt("rt", [128, 2, TB], F32)), 2, "rt")
            rs = Ring(k, ph.enter_context(sbt("rs", [128, 2, TB], F32)), 2, "rs")
            tmp = Ring(k, ph.enter_context(sbt("tmp", [128, 2, TB], F32)), 2, "tmp")
            t1 = Ring(k, ph.enter_context(sbt("t1", [QK, 2, TB], F32)), 2, "t1")
            t2 = Ring(k, ph.enter_context(sbt("t2", [QK, 2, TB], F32)), 2, "t2")
            t2k = ph.enter_context(sbt("t2k", [QK, TB], F32)); B_t2k = k.buf("t2k")
            qo = Ring(k, ph.enter_context(sbt("qo", [QK, 3, TB], BF16)), 3, "qo")
            hn = ph.enter_context(sbt("hn", [128, 8, TB], BF16)); B_hn = k.buf("hn")
            qn = ph.enter_context(sbt("qn", [128, 3, TB], BF16)); B_qn = k.buf("qn")
            kvn = ph.enter_context(sbt("kvn", [128, 2, TB], BF16)); B_kvn = k.buf("kvn")
            kpe = ph.enter_context(sbt("kpe", [32, TB], BF16)); B_kpe = k.buf("kpe")
            vt = Ring(k, ph.enter_context(sbt("vt", [128, 2, NH, 4, 64], BF16)), 2, "vt")
            gu = ph.enter_context(sbt("gu", [64, 4, TB], F32)); B_gu = k.buf("gu")
            gv = Ring(k, ph.enter_context(sbt("gv", [128, 2, 256], F32)), 2, "gv")
            gv2 = Ring(k, ph.enter_context(sbt("gv2", [128, 2, 256], F32)), 2, "gv2")
            vst = Ring(k, ph.enter_context(sbt("vst", [128, 2, 8], F32)), 2, "vst")
            vnb = Ring(k, ph.enter_context(sbt("vnb", [128, 2, 256], BF16)), 2, "vnb")
            sgt = ph.enter_context(sbt("sgt", [64, 4, TB], F32)); B_sgt = k.buf("sgt")
            ysg = Ring(k, ph.enter_context(sbt("ysg", [64, 2, 4, TB], BF16)), 2, "ysg")
            xi = Ring(k, ph.enter_context(sbt("xi", [128, 2, TB], F32)), 2, "xi")
            zt = Ring(k, ph.enter_context(sbt("zt", [128, 2, 2, TB], F32)), 2, "zt")
            bgt = Ring(k, ph.enter_context(sbt("bgt", [128, 2, 2, TB], F32)), 2, "bgt")
            B_qS = k.buf("qS"); B_kI = k.buf("kI"); B_vI = k.buf("vI"); B_sgS = k.buf("sgS")
            B_zS = k.buf("zS"); B_bgS = k.buf("bgS")

            def colsl(c0, n):
                return slice(c0, c0 + n)
            loaded = {}

            def load(i):
                if i < len(blocks) and i not in loaded:
                    hd, T, col0, stream = blocks[i]
                    ha, hb = ht.next()
                    k.dma("sp", ha[:, :, 0:T], hd, (), [hb])
                    ra, rb = rp.next()
                    k.dma("sp", ra[:, :, 0:T], rope[:, :, col0:col0 + T].rearrange("a d t -> d a t"), (), [rb])
                    loaded[i] = (ha, hb, ra, rb)
            load(0)
            for bi, (hd, T, col0, stream) in enumerate(blocks):
                ha, hb, ra, rb = loaded[bi]
                load(bi + 1)
                isctx = stream == 1
                full = not (isctx and last)
                norm_modulate((sq, rt, rs, tmp), ha, hb, T, stream, 4, 3, hn, B_hn)

                def proj(out_ap, out_b, c0, M):
                    for kk in range(8):
                        k.mm(out_ap, wmi[:, kk, c0:c0 + M], hn[:, kk, 0:T], kk == 0, kk == 7, [B_wmi, B_hn], [out_b])

                def latent_norm(c0, nch, gcol, dst, dst_b):
                    pl = [ps_next() for _ in range(nch)]
                    for c in range(nch):
                        proj(pl[c][0][:, 0:T], pl[c][1], c0 + c * 128, 128)
                    pss, pssb = ps_next()
                    for c in range(nch):
                        sq_ap, sq_b = sq.next()
                        k.act(sq_ap[:, 0:T], pl[c][0][:, 0:T], AF.Square, [pl[c][1]], [sq_b])
                        k.mm(pss[:, 0:T], ones_b[:], sq_ap[:, 0:T], c == 0, c == nch - 1, [sq_b, B_ones], [pssb], sig=True)
                    rs_ap, rs_b = rstd_of(pss[:, 0:T], 128, T, 1.0 / (nch * 128), [pssb], rt, rs)
                    for c in range(nch):
                        k.stt(dst[:, c, 0:T], pl[c][0][:, 0:T], vecs[:, gcol + c:gcol + c + 1], rs_ap[:, 0:T], ALU.mult,
                              ALU.mult, [pl[c][1], rs_b, B_vecs], [dst_b])

                def head_finish(pA, pAb, t2_ap, t2_b, gcol, dst_dram, dst_B):
                    sq_ap, sq_b = sq.next()
                    k.act(sq_ap[0:QK, 0:T], pA[0:QK, 0:T], AF.Square, [pAb], [sq_b])
                    pss, pssb = ps_next()
                    k.mm(pss[0:QK, 0:T], ones_b[0:QK, 0:QK], sq_ap[0:QK, 0:T], True, True, [sq_b, B_ones], [pssb])
                    rs_ap, rs_b = rstd_of(pss[0:QK, 0:T], QK, T, 1.0 / QK, [pssb], rt, rs)
                    t1_ap, t1_b = t1.next()
                    k.stt(t1_ap[:, 0:T], pA[0:QK, 0:T], vecs[0:QK, gcol:gcol + 1], ra[:, 0, 0:T], ALU.mult, ALU.mult,
                          [pAb, B_vecs, rb], [t1_b])
                    k.tt(t1_ap[:, 0:T], t1_ap[:, 0:T], t2_ap, ALU.add, [t1_b, t2_b], [t1_b], en="pool")
                    q_ap, q_b = qo.next()
                    k.tt(q_ap[:, 0:T], t1_ap[:, 0:T], rs_ap[0:QK, 0:T], ALU.mult, [t1_b, rs_b], [q_b])
                    k.dma("sp", dst_dram, q_ap[:, 0:T], [q_b], [dst_B])

                if full:
                    latent_norm(0, 3, V_GQL, qn, B_qn)
                latent_norm(384, 2, V_GKV, kvn, B_kvn)
                pk, pkb = ps_next()
                proj(pk[0:32, 0:T], pkb, 640, 32)
                k.act(kpe[:, 0:T], pk[0:32, 0:T], AF.Copy, [pkb], [B_kpe])
                pB, pBb = ps_next()
                k.mm(pB[0:QK, 0:T], sel[:, QK:2 * QK], kpe[:, 0:T], True, True, [B_sel, B_kpe], [pBb])
                k.stt(t2k[:, 0:T], pB[0:QK, 0:T], vecs[0:QK, V_GKHP:V_GKHP + 1], ra[:, 1, 0:T], ALU.mult, ALU.mult,
                      [pBb, B_vecs, rb], [B_t2k])
                jobs = ([("q", h) for h in range(NH)] if full else []) + [("k", h) for h in range(NH)]
                st = {}

                def stage1(ji):
                    kind, h = jobs[ji]
                    pA, pAb = ps_next()
                    d = dict(pA=pA, pAb=pAb)
                    if kind == "q":
                        pB2, pB2b = ps_next()
                        for c in range(3):
                            k.mm(pA[0:QK, 0:T], wqu[:, c, h * QK:(h + 1) * QK], qn[:, c, 0:T], c == 0, c == 2,
                                 [B_wqu, B_qn], [pAb])
                        for c in range(3):
                            k.mm(pB2[0:QK, 0:T], wqu[:, c, NH * QK + h * QK:NH * QK + (h + 1) * QK], qn[:, c, 0:T], c == 0,
                                 c == 2, [B_wqu, B_qn], [pB2b])
                    else:
                        for c in range(2):
                            k.mm(pA[0:QK, 0:T], wkk[:, c, h * QK:(h + 1) * QK], kvn[:, c, 0:T], c == 0, False,
                                 [B_wkk, B_kvn], [pAb])
                        k.mm(pA[0:QK, 0:T], sel[:, 0:QK], kpe[:, 0:T], False, True, [B_sel, B_kpe], [pAb])
                    sq_ap, sq_b = sq.next()
                    k.act(sq_ap[0:QK, 0:T], pA[0:QK, 0:T], AF.Square, [pAb], [sq_b])
                    d["sq"] = (sq_ap, sq_b)
                    if kind == "q":
                        t2_ap, t2_b = t2.next()
                        k.stt(t2_ap[:, 0:T], pB2[0:QK, 0:T], vecs[0:QK, V_GQHP:V_GQHP + 1], ra[:, 1, 0:T], ALU.mult,
                              ALU.mult, [pB2b, B_vecs, rb], [t2_b])
                        d["t2"] = (t2_ap[:, 0:T], t2_b)
                    else:
                        d["t2"] = (t2k[:, 0:T], B_t2k)
                    st[ji] = d

                def stage2(ji):
                    kind, h = jobs[ji]
                    d = st.pop(ji)
                    pA, pAb = d["pA"], d["pAb"]
                    sq_ap, sq_b = d["sq"]
                    pss, pssb = ps_next()
                    k.mm(pss[0:QK, 0:T], ones_b[0:QK, 0:QK], sq_ap[0:QK, 0:T], True, True, [sq_b, B_ones], [pssb])
                    gcol = V_GQH if kind == "q" else V_GKH
                    t1_ap, t1_b = t1.next()
                    k.stt(t1_ap[:, 0:T], pA[0:QK, 0:T], vecs[0:QK, gcol:gcol + 1], ra[:, 0, 0:T], ALU.mult, ALU.mult,
                          [pAb, B_vecs, rb], [t1_b])
                    k.tt(t1_ap[:, 0:T], t1_ap[:, 0:T], d["t2"][0], ALU.add, [t1_b, d["t2"][1]], [t1_b])
                    rs_ap, rs_b = rstd_of(pss[0:QK, 0:T], QK, T, 1.0 / QK, [pssb], rt, rs)
                    q_ap, q_b = qo.next()
                    k.tt(q_ap[:, 0:T], t1_ap[:, 0:T], rs_ap[0:QK, 0:T], ALU.mult, [t1_b, rs_b], [q_b], en="pool")
                    if kind == "q":
                        k.dma("sp", qS[h, :, col0:col0 + T], q_ap[:, 0:T], [q_b], [B_qS])
                    else:
                        k.dma("sp", kI[h // 2][(h % 2) * QK:(h % 2 + 1) * QK, col0:col0 + T], q_ap[:, 0:T], [q_b], [B_kI])

                if full:
                    for ji in range(len(jobs) + 1):
                        if ji < len(jobs):
                            stage1(ji)
                        if ji >= 1:
                            stage2(ji - 1)
                else:
                    for h in range(NH):
                        pA, pAb = ps_next()
                        for c in range(2):
                            k.mm(pA[0:QK, 0:T], wkk[:, c, h * QK:(h + 1) * QK], kvn[:, c, 0:T], c == 0, False,
                                 [B_wkk, B_kvn], [pAb])
                        k.mm(pA[0:QK, 0:T], sel[:, 0:QK], kpe[:, 0:T], False, True, [B_sel, B_kpe], [pAb])
                        head_finish(pA, pAb, t2k[:, 0:T], B_t2k, V_GKH,
                                    kI[h // 2][(h % 2) * QK:(h % 2 + 1) * QK, col0:col0 + T], B_kI)
                va, vb = vt.next()
                for j in range(T // 128):
                    pv, pvb = ps_next()
                    for c in range(2):
                        k.mm(pv[:, :], kvn[:, c, j * 128:(j + 1) * 128], wkv[:, c, :], c == 0, c == 1, [B_kvn, B_wkv], [pvb])
                    k.act(va[:, :, j, :], pv[:, :].rearrange("p (h d) -> p h d", d=64), AF.Copy, [pvb], [vb])
                for c4 in range(4):
                    k.dma("sp", vI[c4].rearrange("(h p) (t d) -> p h t d", p=128, d=64)[:, :, col0 // 128:(col0 + T) // 128, :],
                          va[:, 2 * c4:2 * c4 + 2, 0:T // 128, :], [vb], [B_vI])
                if not full:
                    continue
                for gq in range(4):
                    pu, pub = ps_next()
                    proj(pu[0:64, 0:T], pub, 672 + gq * 64, 64)
                    k.act(gu[:, gq, 0:T], pu[0:64, 0:T], AF.Gelu_apprx_tanh, [pub], [B_gu])
                psg = [ps_next() for _ in range(4)]
                for j in range(T // 128):
                    pv, pvb = ps_next()
                    for kk in range(8):
                        k.mm(pv[:, 0:256], hn[:, kk, j * 128:(j + 1) * 128], wmi[:, kk, 928:1184], kk == 0, kk == 7,
                             [B_hn, B_wmi], [pvb])
                    gv_ap, gv_b = gv.next()
                    k.act(gv_ap[:], pv[:, 0:256], AF.Gelu_apprx_tanh, [pvb], [gv_b])
                    g2_ap, g2_b = gv2.next()
                    k.tt(g2_ap[:], gv_ap[:], gv_ap[:], ALU.mult, [gv_b], [g2_b])
                    st_ap, st_b = vst.next()
                    k.op("dve", lambda e: e.tensor_reduce(out=st_ap[:, 0:4], in_=g2_ap[:].rearrange("p (g d) -> p g d", d=64),
                                                          axis=AX.X, op=ALU.add), [g2_b], [st_b])
                    k.act(st_ap[:, 4:8], st_ap[:, 0:4], AF.Sqrt, [st_b], [st_b], bias=eps_col[:, :], scale=1.0 / 64)
                    k.recip(st_ap[:, 0:4], st_ap[:, 4:8], [st_b], [st_b])
                    k.tt(g2_ap[:].rearrange("p (g d) -> p g d", d=64), gv_ap[:].rearrange("p (g d) -> p g d", d=64),
                         st_ap[:, 0:4].unsqueeze(2).to_broadcast([128, 4, 64]), ALU.mult, [gv_b, st_b], [g2_b])
                    vn_ap, vn_b = vnb.next()
                    k.tt(vn_ap[:], g2_ap[:], gsg[:], ALU.mult, [g2_b, B_gsg], [vn_b])
                    for gq in range(4):
                        k.mm(psg[gq][0][0:64, j * 128:(j + 1) * 128], vn_ap[:, gq * 64:(gq + 1) * 64], wsp[:, gq, :], True,
                             False, [vn_b, B_wsp], [psg[gq][1]])
                        k.mm(psg[gq][0][0:64, j * 128:(j + 1) * 128], ones_f[0:1, 0:64], bsp[:, gq * 128:(gq + 1) * 128],
                             False, True, [B_onesf, B_bsp], [psg[gq][1]])
                for gq in range(4):
                    k.tt(sgt[:, gq, 0:T], psg[gq][0][0:64, 0:T], gu[:, gq, 0:T], ALU.mult, [psg[gq][1], B_gu], [B_sgt])
                pss, pssb = ps_next()
                for gq in range(4):
                    sq_ap, sq_b = sq.next()
                    k.act(sq_ap[0:64, 0:T], sgt[:, gq, 0:T], AF.Square, [B_sgt], [sq_b])
                    k.mm(pss[0:64, 0:T], ones_b[0:64, 0:64], sq_ap[0:64, 0:T], gq == 0, gq == 3, [sq_b, B_ones], [pssb], sig=True)
                rs_ap, rs_b = rstd_of(pss[0:64, 0:T], 64, T, 1.0 / 256, [pssb], rt, rs)
                ys_ap, ys_b = ysg.next()
                for gq in range(4):
                    k.stt(ys_ap[:, gq, 0:T], sgt[:, gq, 0:T], vecs[0:64, V_GOS + gq:V_GOS + gq + 1], rs_ap[0:64, 0:T],
                          ALU.mult, ALU.mult, [B_sgt, B_vecs, rs_b], [ys_b])
                k.dma("sp", sgS[:, :, col0:col0 + T].rearrange("g d t -> d g t"), ys_ap[:, :, 0:T], [ys_b], [B_sgS])
                z_ap, z_b = zt.next()
                bg_ap, bg_b = bgt.next()
                for c in range(2):
                    pc, pcb = ps_next()
                    px, pxb = ps_next()
                    proj(pc[:, 0:T], pcb, 1440 + c * 128, 128)
                    proj(px[:, 0:T], pxb, 1696 + c * 128, 128)
                    xi_ap, xi_b = xi.next()
                    k.act(xi_ap[:, 0:T], px[:, 0:T], AF.Copy, [pxb], [xi_b])
                    k.tt(z_ap[:, c, 0:T], pc[:, 0:T], xi_ap[:, 0:T], ALU.mult, [pcb, xi_b], [z_b])
                    pg, pgb = ps_next()
                    proj(pg[:, 0:T], pgb, 1184 + c * 128, 128)
                    k.act(bg_ap[:, c, 0:T], pg[:, 0:T], AF.Copy, [pgb], [bg_b])
                if isctx:
                    k.dma("sp", zC[:, :, 1:1 + T].rearrange("c p t -> p c t"), z_ap[:, :, 0:T], [z_b], [B_zS])
                else:
                    k.dma("sp", zS[:, :, 1 + col0:1 + col0 + T].rearrange("c p t -> p c t"), z_ap[:, :, 0:T], [z_b], [B_zS])
                k.dma("sp", bgS[:, :, col0:col0 + T].rearrange("c p t -> p c t"), bg_ap[:, :, 0:T], [bg_b], [B_bgS])
            k.barrier()

    def phase_xchg():
        rg = [[2 * i, 2 * i + 1] for i in range(ncores // 2)]
        e = k.engs["pool"]
        B_zhI = k.buf("zhI")
        for c in range(2):
            k.dma("pool", zhI[0:1, c * 128:(c + 1) * 128].rearrange("a p -> p a"), zS[c, :, 1:2], (), [B_zhI], slow=True)
            k.dma("pool", zhI[1:2, c * 128:(c + 1) * 128].rearrange("a p -> p a"), zS[c, :, SEQ_C:SEQ_C + 1], (), [B_zhI], slow=True)
        k.deps(e, [B_zhI], [])
        B_zhO = k.buf("P:zhO")
        for c in range(4):
            xb["k"][c] = k.buf("P:kO")
            xb["v"][c] = k.buf("P:vO")
        order = [(zhI, zhO, B_zhO)]
        for c in range(4):
            order += [(kI[c], kO[c], xb["k"][c]), (vI[c], vO[c], xb["v"][c])]
        for (ci, co, bb) in order:
            sc_ = k.new_sem("cc")
            e.eng.collective_compute("AllGather", ALU.bypass, replica_groups=rg, ins=[ci], outs=[co]).then_inc(sc_.h, 1)
            sc_.count = 1
            bb.w = (sc_, 1)
        for c in range(2):
            k.dma("pool", zh[:, c, 0:1], zhO[1:2, c * 128:(c + 1) * 128].rearrange("a p -> p a"), [B_zhO], [B_zh], slow=True)
            k.dma("pool", zh[:, c, 1:2], zhO[2:3, c * 128:(c + 1) * 128].rearrange("a p -> p a"), [B_zhO], [B_zh], slow=True)
        for c in range(2):
            k.tt(zh[:, c, :], zh[:, c, :], hm[:, :], ALU.mult, [B_zh, B_hm], [B_zh])
        B_zS2 = k.buf("P:zS2")
        for c in range(2):
            k.dma("pool", zS[c, :, 0:1], zh[:, c, 0:1], [B_zh], [B_zS2], slow=True)
            k.dma("pool", zS[c, :, SEQ_C + 1:SEQ_C + 2], zh[:, c, 1:2], [B_zh], [B_zS2], slow=True)

    def phase_attn(l, last):
        scale = float(QK) ** -0.5
        with ExitStack() as ph:
            kT = Ring(k, ph.enter_context(sbt("kT", [QK, 2, 2 * SEQ_C + CTX], BF16)), 2, "kT")
            va = ph.enter_context(sbt("va", [128, 2, 66, 128], BF16))
            B_va = [k.buf("va0"), k.buf("va1")]
            B_vones = k.buf("vones")
            NQ = 4
            qT = Ring(k, ph.enter_context(sbt("qT", [QK, NQ, TB], BF16)), NQ, "qT")
            pT = Ring(k, ph.enter_context(sbt("pT", [128, 3, 3, TB], BF16)), 3, "pT")
            den = Ring(k, ph.enter_context(sbt("den", [64, 2, TB], F32)), 2, "den")
            ao = Ring(k, ph.enter_context(sbt("ao", [64, 2, TB], F32)), 2, "ao")
            B_aS = k.buf("aS")
            k.memset(va[:, :, :, 64:128], 1.0, [B_vones])
            S_B = [k.buf("S0"), k.buf("S1")]
            O_B = [k.buf("O0"), k.buf("O1")]
            for b_ in S_B + O_B:
                b_.excl = True
            lat_keys = [(t * 128, t) for t in range(34)] + [(NTOK + t * 128, 34 + t) for t in range(32)]
            ctx_keys = [(SEQ_C + t * 128, 32 + t) for t in range(2)]
            tasks = []
            for h in range(NH):
                for (hd, T, col0, stream) in blocks:
                    if stream == 1 and last:
                        continue
                    tasks.append(dict(h=h, T=T, col0=col0, keys=lat_keys if stream == 0 else ctx_keys))
            items = []
            for ti, t in enumerate(tasks):
                groups = [t["keys"][i:i + 3] for i in range(0, len(t["keys"]), 3)]
                for gi, gk in enumerate(groups):
                    items.append((ti, gi, len(groups), gk))
            heads = {}

            def load_head(h):
                if h < NH and h not in heads:
                    kt_ap, kt_b = kT.next()
                    vi = h % 2
                    hh = h % 2
                    k.dma("sp", kt_ap[:, 0:NTOK], kO[h // 2][hh * QK:(hh + 1) * QK, :], [xb["k"][h // 2]], [kt_b])
                    k.dma("sp", kt_ap[:, NTOK:NTOK + SEQ_C], kO[h // 2][2 * QK + hh * QK:2 * QK + (hh + 1) * QK, 0:SEQ_C], [xb["k"][h // 2]], [kt_b])
                    for (r0, t0, nt) in ((0, 0, 17), (0, 17, 17), (1, 0, 16), (1, 16, 16)):
                        k.dma("pool", va[:, vi, 34 * r0 + t0:34 * r0 + t0 + nt, 0:64],
                              vO[h // 2][r0 * 256 + (h % 2) * 128:r0 * 256 + (h % 2 + 1) * 128, t0 * 64:(t0 + nt) * 64]
                              .rearrange("p (t d) -> p t d", d=64), [xb["v"][h // 2]], [B_va[vi]])
                    heads[h] = (kt_ap, kt_b, vi)
            ql = {}

            def load_q(ti):
                if ti < len(tasks) and ti not in ql:
                    t = tasks[ti]
                    q_ap, q_b = qT.next()
                    k.dma("sp", q_ap[:, 0:t["T"]], qS[t["h"], :, t["col0"]:t["col0"] + t["T"]], (), [q_b])
                    ql[ti] = (q_ap, q_b)

            def prep_task(ti):
                if ti < len(tasks):
                    h = tasks[ti]["h"]
                    if ti == 0:
                        load_head(h)
                    for d in range(NQ - 1):
                        load_q(ti + d)

            def emit_S(idx):
                ti, gi, ng, gk = items[idx]
                t = tasks[ti]
                if gi == 0:
                    prep_task(ti)
                kt_ap, kt_b, vi = heads[t["h"]]
                q_ap, q_b = ql[ti]
                sslot = idx % 2
                for tj, (kc, vtile) in enumerate(gk):
                    k.mm(psum[:, 3 * sslot + tj, 0:t["T"]], kt_ap[:, kc:kc + 128], q_ap[:, 0:t["T"]], True, True,
                         [kt_b, q_b], [S_B[sslot]], sig=(tj == len(gk) - 1))

            def emit_E_PV(idx):
                ti, gi, ng, gk = items[idx]
                t = tasks[ti]
                T = t["T"]
                kt_ap, kt_b, vi = heads[t["h"]]
                sslot = idx % 2
                n = len(gk)
                p_ap, p_b = pT.next()
                k.act(p_ap[:, 0:n, 0:T], psum[:, 3 * sslot:3 * sslot + n, 0:T], AF.Exp, [S_B[sslot], B_nshift], [p_b],
                      bias=nshift[:, :], scale=scale)
                o = ti % 2
                po, pob = psum[:, 6 + o], O_B[o]
                if gi == 0 and (ti == 0 or tasks[ti - 1]["h"] != t["h"]):
                    load_head(t["h"] + 1)
                for tj, (kc, vtile) in enumerate(gk):
                    first = gi == 0 and tj == 0
                    lastmm = gi == ng - 1 and tj == n - 1
                    k.mm(po[:, 0:T], va[:, vi, vtile, :], p_ap[:, tj, 0:T], first, lastmm,
                         [B_va[vi], B_vones, p_b], [pob])
                if gi == ng - 1:
                    d_ap, d_b = den.next()
                    k.copy(d_ap[:, 0:T], po[64:128, 0:T], [pob], [d_b])
                    k.recip(d_ap[:, 0:T], d_ap[:, 0:T], [d_b], [d_b])
                    a_ap, a_b = ao.next()
                    k.tt(a_ap[:, 0:T], po[0:64, 0:T], d_ap[:, 0:T], ALU.mult, [pob, d_b], [a_b])
                    k.dma("sp", aS[t["h"], :, t["col0"]:t["col0"] + T], a_ap[:, 0:T], [a_b], [B_aS])

            emit_S(0)
            for idx in range(len(items)):
                if idx + 1 < len(items):
                    emit_S(idx + 1)
                emit_E_PV(idx)
            k.barrier()

    def phase_merge(l, last):
        with ExitStack() as ph:
            wA = ph.enter_context(sbt("wA", [64, NH, D], BF16)); B_wA = k.buf("wA")
            wS = ph.enter_context(sbt("wS", [64, 4, D], BF16)); B_wS = k.buf("wS")
            wC = ph.enter_context(sbt("wC", [128, 2, D], BF16)); B_wC = k.buf("wC")
            k.dma("pool", wA[:], w_mo[l, 0:512, :].rearrange("(h p) n -> p h n", p=64), (), [B_wA])
            k.dma("pool", wS[:], w_mo[l, 512:768, :].rearrange("(h p) n -> p h n", p=64), (), [B_wS])
            k.dma("pool", wC[:], w_mo[l, 768:1024, :].rearrange("(h p) n -> p h n", p=128), (), [B_wC])
            nht = 3 if last else 2
            ht = Ring(k, ph.enter_context(sbt("ht", [128, nht, 8, TB], F32)), nht, "ht")
            at = Ring(k, ph.enter_context(sbt("at", [64, 2, NH, TB], F32)), 2, "at")
            ys = Ring(k, ph.enter_context(sbt("ys", [64, 2, 4, TB], BF16)), 2, "ys")
            zt = Ring(k, ph.enter_context(sbt("zt", [128, 2, 2, TB + 2], F32)), 2, "zt")
            bgt = Ring(k, ph.enter_context(sbt("bgt", [128, 2, 2, TB], F32)), 2, "bgt")
            sq = Ring(k, ph.enter_context(sbt("sq", [128, 2, TB], BF16)), 2, "sq")
            rt = Ring(k, ph.enter_context(sbt("rt", [128, 2, TB], F32)), 2, "rt")
            rs = Ring(k, ph.enter_context(sbt("rs", [128, 2, TB], F32)), 2, "rs")
            ya = Ring(k, ph.enter_context(sbt("ya", [64, 2, NH, TB], BF16)), 2, "ya")
            cvt = ph.enter_context(sbt("cvt", [128, 2, TB], F32)); B_cvt = k.buf("cvt")
            yc = Ring(k, ph.enter_context(sbt("yc", [128, 2, 2, TB], BF16)), 2, "yc")
            blks = [b for b in blocks if not (b[3] == 1 and last)]
            loaded = {}

            def load(i):
                if i < len(blks) and i not in loaded:
                    hd, T, col0, stream = blks[i]
                    ha, hb = ht.next()
                    k.dma("sp", ha[:, :, 0:T], hd, (), [hb])
                    a_ap, a_b = at.next()
                    k.dma("sp", a_ap[:, :, 0:T], aS[:, :, col0:col0 + T].rearrange("h d t -> d h t"), (), [a_b])
                    y_ap, y_b = ys.next()
                    k.dma("pool", y_ap[:, :, 0:T], sgS[:, :, col0:col0 + T].rearrange("g d t -> d g t"), (), [y_b])
                    z_ap, z_b = zt.next()
                    if stream == 1:
                        k.dma("pool", z_ap[:, :, 0:T + 2], zC.rearrange("c p t -> p c t"), (), [z_b])
                    else:
                        k.dma("pool", z_ap[:, :, 0:T + 2], zS[:, :, col0:col0 + T + 2].rearrange("c p t -> p c t"), (), [z_b])
                    g_ap, g_b = bgt.next()
                    k.dma("pool", g_ap[:, :, 0:T], bgS[:, :, col0:col0 + T].rearrange("c p t -> p c t"), (), [g_b])
                    loaded[i] = (ha, hb, a_ap, a_b, y_ap, y_b, z_ap, z_b, g_ap, g_b)
            load(0)

            def prep(bi):
                hd, T, col0, stream = blks[bi]
                ha, hb, a_ap, a_b, y_ap, y_b, z_ap, z_b, g_ap, g_b = loaded[bi]
                ya_ap, ya_b = ya.next()
                yc_ap, yc_b = yc.next()
                pss, pssb = ps_next()
                for h in range(NH):
                    sq_ap, sq_b = sq.next()
                    k.act(sq_ap[0:64, 0:T], a_ap[:, h, 0:T], AF.Square, [a_b], [sq_b])
                    k.mm(pss[0:64, 0:T], ones_b[0:64, 0:64], sq_ap[0:64, 0:T], h == 0, h == NH - 1, [sq_b, B_ones], [pssb], sig=True)
                rs_ap, rs_b = rstd_of(pss[0:64, 0:T], 64, T, 1.0 / 512, [pssb], rt, rs)
                for h in range(NH):
                    k.stt(ya_ap[:, h, 0:T], a_ap[:, h, 0:T], vecs[0:64, V_GOA + h:V_GOA + h + 1], rs_ap[0:64, 0:T], ALU.mult,
                          ALU.mult, [a_b, B_vecs, rs_b], [ya_b])
                for c in range(2):
                    wc = V_WCV + c * 3
                    k.ts(cvt[:, c, 0:T], z_ap[:, c, 1:T + 1], vecs[:, wc + 1:wc + 2], None, ALU.mult, None,
                         [z_b, B_vecs], [B_cvt])
                    k.stt(cvt[:, c, 0:T], z_ap[:, c, 0:T], vecs[:, wc:wc + 1], cvt[:, c, 0:T], ALU.mult, ALU.add,
                          [z_b, B_vecs, B_cvt], [B_cvt])
                    k.stt(cvt[:, c, 0:T], z_ap[:, c, 2:T + 2], vecs[:, wc + 2:wc + 3], cvt[:, c, 0:T], ALU.mult, ALU.add,
                          [z_b, B_vecs, B_cvt], [B_cvt])
                    k.tt(cvt[:, c, 0:T], cvt[:, c, 0:T], g_ap[:, c, 0:T], ALU.mult, [B_cvt, g_b], [B_cvt], en="pool")
                pss, pssb = ps_next()
                for c in range(2):
                    sq_ap, sq_b = sq.next()
                    k.act(sq_ap[:, 0:T], cvt[:, c, 0:T], AF.Square, [B_cvt], [sq_b])
                    k.mm(pss[:, 0:T], ones_b[:], sq_ap[:, 0:T], c == 0, c == 1, [sq_b, B_ones], [pssb], sig=True)
                rs_ap, rs_b = rstd_of(pss[:, 0:T], 128, T, 1.0 / 256, [pssb], rt, rs)
                for c in range(2):
                    k.stt(yc_ap[:, c, 0:T], cvt[:, c, 0:T], vecs[:, V_GOC + c:V_GOC + c + 1], rs_ap[:, 0:T], ALU.mult, ALU.mult,
                          [B_cvt, B_vecs, rs_b], [yc_b])
                preps[bi] = (ya_ap, ya_b, yc_ap, yc_b)

            def main_mm(bi, ocs):
                hd, T, col0, stream = blks[bi]
                y_ap, y_b = loaded[bi][4], loaded[bi][5]
                ya_ap, ya_b, yc_ap, yc_b = preps[bi]
                res = []
                for oc in ocs:
                    po, pob = ps_next()
                    osl = slice(oc * 128, (oc + 1) * 128)
                    for h in range(NH):
                        k.mm(po[:, 0:T], wA[:, h, osl], ya_ap[:, h, 0:T], h == 0, False, [B_wA, ya_b], [pob])
                    for gq in range(4):
                        k.mm(po[:, 0:T], wS[:, gq, osl], y_ap[:, gq, 0:T], False, False, [B_wS, y_b], [pob])
                    for c in range(2):
                        k.mm(po[:, 0:T], wC[:, c, osl], yc_ap[:, c, 0:T], False, c == 1, [B_wC, yc_b], [pob])
                    res.append((oc, po, pob))
                return res

            def main_upd(bi, res):
                hd, T, col0, stream = blks[bi]
                ha, hb = loaded[bi][0], loaded[bi][1]
                for (oc, po, pob) in res:
                    k.stt(ha[:, oc, 0:T], po[:, 0:T], der[:, stream, 5 * 8 + oc:5 * 8 + oc + 1], ha[:, oc, 0:T], ALU.mult,
                          ALU.add, [pob, B_der, hb], [hb])

            preps = {}
            ada = None
            if not last:
                ada = AdaJob(l + 1, ph)
                ada.dma(0)
            prep(0)
            for bi, (hd, T, col0, stream) in enumerate(blks):
                load(bi + 1)
                if ada is not None:
                    ada.step(bi)
                ra_ = main_mm(bi, range(0, 4))
                if bi + 1 < len(blks):
                    prep(bi + 1)
                main_upd(bi, ra_)
                rb_ = main_mm(bi, range(4, 8))
                main_upd(bi, rb_)
                k.dma("sp", hd, loaded[bi][0][:, :, 0:T], [loaded[bi][1]], [k.buf("hS")])
            if ada is not None:
                ada.finish()
            k.barrier()

    eps_col = sb("eps_col", [128, 1]); B_eps = k.buf("P:eps")
    k.memset(eps_col[:], EPS, [B_eps])
    k.barrier()

    plan = []
    plan.append(("tin", phase_tin))
    for l in range(DEPTH):
        last = l == DEPTH - 1
        plan.append((f"set{l}", lambda l=l: set_layer(l)))
        plan.append((f"f1_{l}", lambda l=l: phase_ffn(l, 0, (0, 1))))
        plan.append((f"mix{l}", lambda l=l, last=last: phase_mix(l, last)))
        plan.append((f"x{l}", phase_xchg))
        plan.append((f"att{l}", lambda l=l, last=last: phase_attn(l, last)))
        plan.append((f"mrg{l}", lambda l=l, last=last: phase_merge(l, last)))
        plan.append((f"f2_{l}", lambda l=l, last=last: phase_ffn(l, 1, (0,) if last else (0, 1))))
    plan.append(("tout", phase_tout))
    for name, fn in plan:
        fn()
        if stop_after == name:
            break
    k.barrier()
    es.close()
    return nc


def _rope_tables(s):
    pos = np.arange(s * SEQ_C, (s + 1) * SEQ_C)
    row = (pos // 64).astype(np.float32)
    col = (pos % 64).astype(np.float32)
    inv = (1.0 / (np.float32(10000.0) ** (np.arange(0, 16, 2, dtype=np.float32) / np.float32(16)))).astype(np.float32)
    ang_r = row[:, None] * inv
    ang_c = col[:, None] * inv
    ang = np.concatenate([ang_r, ang_r, ang_c, ang_c], axis=-1).astype(np.float32)
    cos = np.cos(ang).astype(np.float32).T
    sin = np.sin(ang).astype(np.float32).T
    sign = np.ones((32, 1), np.float32)
    sign[0:8] = -1.0
    sign[16:24] = -1.0
    t = np.zeros((2, QK, NTOK), np.float32)
    t[0, :, :] = 1.0
    t[0, 64:96, 0:SEQ_C] = cos
    t[1, 64:96, 0:SEQ_C] = sin * sign
    return t


def _perm():
    p = np.arange(QK)
    for base in (64, 80):
        for i in range(8):
            p[base + i] = base + 8 + i
            p[base + 8 + i] = base + i
    return p


def _prep_shared(inp):
    f = lambda a: np.ascontiguousarray(np.asarray(a, dtype=np.float32))
    L = DEPTH
    perm = _perm()
    sh = {}
    sh["w_ada"] = f(inp["w_ada"])
    sh["b_ada_t"] = f(np.asarray(inp["b_ada"]).reshape(L, 72, 128).transpose(0, 2, 1))
    for n in ("w_ffn1_in", "w_ffn2_in", "w_ffn1_out", "w_ffn2_out", "w_mix_in", "w_mix_out"):
        sh[n] = f(inp[n])
    wq = np.asarray(inp["w_q_up"]).reshape(L, 384, NH, QK)
    wqp = wq[:, :, :, perm].copy()
    wqp[:, :, :, 0:64] = 0.0
    sh["w_qu"] = f(np.concatenate([wq.reshape(L, 384, NH * QK), wqp.reshape(L, 384, NH * QK)], axis=-1))
    wkv = np.asarray(inp["w_kv_up"]).reshape(L, 256, NH, 128)
    wkk = np.zeros((L, 256, NH, QK), np.float32)
    wkk[..., 0:64] = wkv[..., 0:64]
    sh["w_kvk"] = f(wkk.reshape(L, 256, NH * QK))
    sh["w_kvv"] = f(wkv[..., 64:128].reshape(L, 256, NH * 64))
    sh["w_spT"] = f(np.asarray(inp["w_spatial"]).transpose(0, 1, 3, 2))
    sh["b_sp"] = f(np.asarray(inp["b_spatial"]).reshape(L, 1, 512))
    vec = np.zeros((L, 128, NVEC), np.float32)
    colmaj = lambda v, n: np.asarray(v).reshape(L, n, 128).transpose(0, 2, 1)
    vec[:, :, V_GF1:V_GF1 + 8] = colmaj(inp["g_ffn1"], 8)
    vec[:, :, V_GMIX:V_GMIX + 8] = colmaj(inp["g_mix"], 8)
    vec[:, :, V_GF2:V_GF2 + 8] = colmaj(inp["g_ffn2"], 8)
    vec[:, :, V_GQL:V_GQL + 3] = colmaj(inp["g_q_lat"], 3)
    vec[:, :, V_GKV:V_GKV + 2] = colmaj(inp["g_kv_lat"], 2)
    gq = np.asarray(inp["g_q_head"]); gk = np.asarray(inp["g_k_head"])
    vec[:, 0:QK, V_GQH] = gq
    vec[:, 0:QK, V_GQHP] = gq[:, perm]
    vec[:, 0:QK, V_GKH] = gk
    vec[:, 0:QK, V_GKHP] = gk[:, perm]
    go = np.asarray(inp["g_out"])
    vec[:, 0:64, V_GOA:V_GOA + 8] = go[:, 0:512].reshape(L, 8, 64).transpose(0, 2, 1)
    vec[:, 0:64, V_GOS:V_GOS + 4] = go[:, 512:768].reshape(L, 4, 64).transpose(0, 2, 1)
    vec[:, :, V_GOC:V_GOC + 2] = go[:, 768:1024].reshape(L, 2, 128).transpose(0, 2, 1)
    wc = np.asarray(inp["w_conv"])
    for c in range(2):
        for kk in range(3):
            vec[:, :, V_WCV + c * 3 + kk] = wc[:, kk, c * 128:(c + 1) * 128]
    sh["vecs"] = f(vec)
    sh["gsgu_b"] = f(np.broadcast_to(np.asarray(inp["g_sgu"]).reshape(L, 1, 256), (L, 128, 256)))
    sh["grow"] = f(np.concatenate([gq, gk], axis=-1).reshape(L, 1, 2 * QK))
    sh["ident"] = np.eye(128, dtype=np.float32)
    sel = np.zeros((32, 2 * QK), np.float32)
    for i in range(32):
        sel[i, 64 + i] = 1.0
    for m in range(64, QK):
        sel[perm[m] - 64, QK + m] = 1.0
    sh["sel"] = sel
    return sh


def make_in_maps(inp):
    sh = _prep_shared(inp)
    x = np.asarray(inp["x"], dtype=np.float32)
    c = np.asarray(inp["c"], dtype=np.float32)
    ctx = np.asarray(inp["ctx"], dtype=np.float32)
    c_ctx = np.asarray(inp["c_ctx"], dtype=np.float32)
    ropes = [_rope_tables(0), _rope_tables(1)]
    maps = []
    for i in range(8):
        b, s = i // 2, i % 2
        m = dict(sh)
        m["x"] = np.ascontiguousarray(x[b, s * SEQ_C:(s + 1) * SEQ_C, :])
        m["ctx"] = np.ascontiguousarray(ctx[b])
        ccv = np.stack([c[b].reshape(8, 128).T, c_ctx.reshape(8, 128).T], axis=-1)
        m["cc"] = np.ascontiguousarray(ccv.astype(np.float32))
        m["rope"] = ropes[s]
        hmk = np.zeros((128, 2), np.float32)
        hmk[:, 0] = 1.0 if s == 1 else 0.0
        hmk[:, 1] = 1.0 if s == 0 else 0.0
        m["hmask"] = hmk
        maps.append(m)
    return maps


_NC_CACHE = {}


def kernel(**inputs):
    if "nc" not in _NC_CACHE:
        _NC_CACHE["nc"] = build_program()
    nc = _NC_CACHE["nc"]
    in_maps = make_in_maps(inputs)
    res = run_bass_kernel_spmd(nc, in_maps, core_ids=list(range(8)))
    outp = np.empty((4, 2 * SEQ_C, D), np.float32)
    for i in range(8):
        b, s = i // 2, i % 2
        outp[b, s * SEQ_C:(s + 1) * SEQ_C, :] = res.results[i]["out"]
    return outp
```

```python
import numpy as np
from contextlib import ExitStack
import concourse.bass as bass
import concourse.mybir as mybir
from concourse.bass_utils import run_bass_kernel_spmd

F32 = mybir.dt.float32
BF16 = mybir.dt.bfloat16
AF = mybir.ActivationFunctionType
ALU = mybir.AluOpType
AX = mybir.AxisListType

D = 1024
DFF = 2816
NJ = 22
SEQ_C = 4096
CTX = 256
NTOK = SEQ_C + CTX
TB = 512
EPS = 1e-6
DEPTH = 2
NH = 8
QK = 96
MIXC = 1952
JGROUPS = [(0, 6), (6, 12), (12, 17), (17, 22)]
V_GF1, V_GMIX, V_GF2, V_GQL, V_GKV, V_GQH, V_GQHP, V_GKH, V_GKHP, V_GOA, V_GOS, V_GOC, V_WCV = \
    0, 8, 16, 24, 27, 29, 30, 31, 32, 33, 41, 45, 47
NVEC = 53


class Sem:
    _n = 0

    def __init__(self, h):
        self.h = h
        Sem._n += 1
        self.key = Sem._n
        self.count = 0


class Buf:
    __slots__ = ("name", "w", "r", "dsem", "excl")

    def __init__(self, name=""):
        self.name = name
        self.w = None
        self.r = {}
        self.dsem = None
        self.excl = False


class Eng:
    def __init__(self, name, eng):
        self.name = name
        self.eng = eng
        self.sem = None
        self.seen = {}


class K:
    def __init__(self, nc, es):
        self.nc = nc
        self.es = es
        self.engs = {"pe": Eng("pe", nc.tensor), "act": Eng("act", nc.scalar), "dve": Eng("dve", nc.vector),
                     "pool": Eng("pool", nc.gpsimd), "sp": Eng("sp", nc.sync)}
        self.free_dsems = {}
        self.live_dsems = []
        self.bufs = []
        self.nsem = 0
        self.new_epoch()

    def new_sem(self, name):
        self.nsem += 1
        return Sem(self.es.enter_context(self.nc.semaphore(f"{name}_{self.nsem}")))

    def new_epoch(self):
        for e in self.engs.values():
            e.sem = self.new_sem("e" + e.name)
            e.seen = {}

    def buf(self, name=""):
        b = Buf(name)
        self.bufs.append(b)
        return b

    def wait(self, e, stamp):
        sem, val = stamp
        if e.name == "pe" and sem is e.sem:
            return
        if e.seen.get(sem.key, 0) < val:
            e.eng.wait_ge(sem.h, val)
            e.seen[sem.key] = val

    def deps(self, e, reads, writes):
        for b in reads:
            if b.w is not None:
                self.wait(e, b.w)
            if b.excl:
                for st in b.r.values():
                    if st[0] is not e.sem:
                        self.wait(e, st)
        for b in writes:
            if b.w is not None and b.w[0] is not e.sem:
                self.wait(e, b.w)
            for st in b.r.values():
                self.wait(e, st)

    def mark(self, stamp, reads, writes):
        sk = stamp[0].key
        for b in reads:
            old = b.r.get(sk)
            if old is None or old[1] < stamp[1]:
                b.r[sk] = stamp
        for b in writes:
            b.w = stamp
            b.r = {}

    def op(self, en, fn, reads=(), writes=(), signal=True):
        e = self.engs[en]
        self.deps(e, reads, writes)
        inst = fn(e.eng)
        if signal:
            e.sem.count += 1
            inst.then_inc(e.sem.h, 1)
            stamp = (e.sem, e.sem.count)
        else:
            stamp = (e.sem, e.sem.count + 1)
        self.mark(stamp, reads, writes)
        return inst

    def dsem_for(self, b, qn):
        if b.dsem is None:
            fp = self.free_dsems.setdefault(qn, [])
            b.dsem = fp.pop() if fp else self.new_sem("d" + qn)
            b.dsem.q = qn
            self.live_dsems.append(b)
        assert b.dsem.q == qn, (b.name, b.dsem.q, qn)
        return b.dsem

    def dma(self, qn, out, in_, reads=(), writes=(), slow=False):
        e = self.engs[qn]
        self.deps(e, reads, writes)
        if slow:
            inst = e.eng.dma_start(out=out, in_=in_, allow_slow_non_contiguous=True)
        else:
            inst = e.eng.dma_start(out=out, in_=in_)
        s = self.dsem_for(writes[0], qn)
        s.count += 16
        inst.then_inc(s.h, 16)
        self.mark((s, s.count), reads, writes)
        return inst

    def barrier(self):
        es = list(self.engs.values())
        stamps = [(e.sem, e.sem.count) for e in es if e.sem.count > 0]
        for b in self.live_dsems:
            stamps.append((b.dsem, b.dsem.count))
        for e in es:
            for st in stamps:
                if st[0] is not e.sem:
                    self.wait(e, st)
        for b in self.live_dsems:
            self.free_dsems[b.dsem.q].append(b.dsem)
            b.dsem = None
        self.live_dsems = []
        for b in self.bufs:
            b.w = None
            b.r = {}
        self.bufs = [b for b in self.bufs if b.name.startswith("P:")]

    def mm(self, out, lhsT, rhs, start, stop, R, W, sig=None):
        return self.op("pe", lambda e: e.matmul(out, lhsT, rhs, start=start, stop=stop), R, W,
                       signal=(stop if sig is None else sig))

    def act(self, out, in_, func, R, W, bias=0.0, scale=1.0):
        return self.op("act", lambda e: e.activation(out=out, in_=in_, func=func, bias=bias, scale=scale), R, W)

    def tt(self, out, in0, in1, op, R, W, en="dve"):
        return self.op(en, lambda e: e.tensor_tensor(out=out, in0=in0, in1=in1, op=op), R, W)

    def stt(self, out, in0, scalar, in1, op0, op1, R, W, en="dve"):
        return self.op(en, lambda e: e.scalar_tensor_tensor(out=out, in0=in0, scalar=scalar, in1=in1,
                                                            op0=op0, op1=op1), R, W)

    def ts(self, out, in0, s1, s2, op0, op1, R, W, en="dve"):
        if s2 is None:
            return self.op(en, lambda e: e.tensor_scalar(out=out, in0=in0, scalar1=s1, scalar2=None, op0=op0), R, W)
        return self.op(en, lambda e: e.tensor_scalar(out=out, in0=in0, scalar1=s1, scalar2=s2, op0=op0, op1=op1),
                       R, W)

    def copy(self, out, in_, R, W, en="dve"):
        return self.op(en, lambda e: e.tensor_copy(out=out, in_=in_), R, W)

    def recip(self, out, in_, R, W):
        return self.op("dve", lambda e: e.reciprocal(out=out, in_=in_), R, W)

    def memset(self, ap, val, W, en="dve"):
        return self.op(en, lambda e: e.memset(ap, val), (), W)


class Ring:
    def __init__(self, k, t, n, name):
        self.t = t
        self.n = n
        self.bufs = [k.buf(f"{name}{i}") for i in range(n)]
        self.i = -1

    def next(self):
        self.i += 1
        j = self.i % self.n
        return self.t[:, j], self.bufs[j]


def build_program(debug=False, stop_after=None, ncores=8):
    nc = bass.Bass("TRN2", target_bir_lowering=False)
    es = ExitStack()

    def din(name, shape, dt=F32):
        return nc.dram_tensor(name, list(shape), dt, kind="ExternalInput").ap()

    dbg_kind = "ExternalOutput" if debug else None

    def dscr(name, shape, dt=F32, cc=False):
        if debug and not cc:
            return nc.dram_tensor(name, list(shape), dt, kind="ExternalOutput").ap()
        return nc.dram_tensor(name, list(shape), dt).ap()

    x = din("x", [SEQ_C, D])
    ctxin = din("ctx", [CTX, D])
    cc = din("cc", [128, 8, 2])
    rope = din("rope", [2, QK, NTOK])
    hmask = din("hmask", [128, 2])
    ident_d = din("ident", [128, 128])
    sel_d = din("sel", [32, 2 * QK])
    w_ada = din("w_ada", [DEPTH, D, 9 * D])
    b_ada_t = din("b_ada_t", [DEPTH, 128, 72])
    w_f_in = [din("w_ffn1_in", [DEPTH, D, 2 * DFF]), din("w_ffn2_in", [DEPTH, D, 2 * DFF])]
    w_f_out = [din("w_ffn1_out", [DEPTH, DFF, D]), din("w_ffn2_out", [DEPTH, DFF, D])]
    w_mi = din("w_mix_in", [DEPTH, D, MIXC])
    w_qu = din("w_qu", [DEPTH, 384, 2 * NH * QK])
    w_kvk = din("w_kvk", [DEPTH, 256, NH * QK])
    w_kvv = din("w_kvv", [DEPTH, 256, NH * 64])
    w_spT = din("w_spT", [DEPTH, 4, 128, 128])
    b_sp = din("b_sp", [DEPTH, 1, 512])
    w_mo = din("w_mix_out", [DEPTH, D, D])
    vecs_d = din("vecs", [DEPTH, 128, NVEC])
    gsgu_d = din("gsgu_b", [DEPTH, 128, 256])
    grow_d = din("grow", [DEPTH, 1, 2 * QK])
    out = nc.dram_tensor("out", [SEQ_C, D], F32, kind="ExternalOutput").ap()

    hS = dscr("hS", [8, 128, 8, TB])
    hC = dscr("hC", [128, 8, CTX])
    qS = dscr("qS", [NH, QK, NTOK], BF16)
    kI = [dscr(f"kI{c}", [2 * QK, NTOK], BF16, cc=True) for c in range(4)]
    kO = [dscr(f"kO{c}", [4 * QK, NTOK], BF16, cc=True) for c in range(4)]
    vI = [dscr(f"vI{c}", [2 * 128, 34 * 64], BF16, cc=True) for c in range(4)]
    vO = [dscr(f"vO{c}", [4 * 128, 34 * 64], BF16, cc=True) for c in range(4)]
    aS = dscr("aS", [NH, 64, NTOK])
    sgS = dscr("sgS", [4, 64, NTOK], BF16)
    zS = dscr("zS", [2, 128, SEQ_C + 2])
    zC = dscr("zC", [2, 128, CTX + 2])
    bgS = dscr("bgS", [2, 128, NTOK])
    zhI = dscr("zhI", [2, 256], cc=True)
    zhO = dscr("zhO", [4, 256], cc=True)

    k = K(nc, es)
    cc_sem = k.new_sem("cc")
    cc_count = [0]

    ucnt = [0]

    def sbt(name, shape, dt=F32):
        ucnt[0] += 1
        return nc.sbuf_tensor(f"{name}_u{ucnt[0]}", list(shape), dt)

    def sb(name, shape, dt=F32):
        return es.enter_context(sbt(name, shape, dt))

    ident = sb("ident", [128, 128]);           B_ident = k.buf("P:ident")
    ones_b = sb("ones_b", [128, 128], BF16);   B_ones = k.buf("P:ones")
    ones_f = sb("ones_f", [1, 128]);           B_onesf = k.buf("P:onesf")
    sel = sb("sel", [32, 2 * QK], BF16);       B_sel = k.buf("P:sel")
    scb = sb("scb", [128, 8, 2], BF16);        B_scb = k.buf("P:scb")
    modv_l = [sb("modv", [128, 2, 72]) for _ in range(DEPTH)];   B_modv_l = [k.buf("P:modv") for _ in range(DEPTH)]
    der_l = [sb("der", [128, 2, 72]) for _ in range(DEPTH)];     B_der_l = [k.buf("P:der") for _ in range(DEPTH)]
    vecs_l = [sb("vecs", [128, NVEC]) for _ in range(DEPTH)];    B_vecs_l = [k.buf("P:vecs") for _ in range(DEPTH)]
    modv, der, vecs = modv_l[0], der_l[0], vecs_l[0]
    B_modv, B_der, B_vecs = B_modv_l[0], B_der_l[0], B_vecs_l[0]

    def set_layer(l):
        nonlocal modv, der, vecs, B_modv, B_der, B_vecs
        modv, der, vecs = modv_l[l], der_l[l], vecs_l[l]
        B_modv, B_der, B_vecs = B_modv_l[l], B_der_l[l], B_vecs_l[l]
    hm = sb("hm", [128, 2]);                   B_hm = k.buf("P:hm")
    nshift = sb("nshift", [128, 1]);           B_nshift = k.buf("P:nshift")
    zero_t = sb("zero_t", [128, 2]);           B_zero = k.buf("P:zero")
    zh = sb("zh", [128, 2, 2]);               B_zh = k.buf("P:zh")
    xb = {"k": [None] * 4, "v": [None] * 4}
    psum = es.enter_context(nc.psum_tensor("psum", [128, 8, TB], F32))
    PB = [k.buf(f"P:ps{i}") for i in range(8)]
    for b_ in PB:
        b_.excl = True
    ps_i = [-1]

    ps_reserved = set()

    def ps_next():
        while True:
            ps_i[0] += 1
            j = ps_i[0] % 8
            if j not in ps_reserved:
                return psum[:, j], PB[j]

    blocks = [(hS[i], TB, i * TB, 0) for i in range(8)] + [(hC, CTX, SEQ_C, 1)]

    k.dma("sp", ident[:], ident_d, (), [B_ident])
    k.dma("pool", sel[:], sel_d, (), [B_sel])
    k.dma("sp", hm[:], hmask, (), [B_hm])
    k.memset(ones_b[:], 1.0, [B_ones])
    k.memset(ones_f[:], 1.0, [B_onesf])
    k.memset(zero_t[:], 0.0, [B_zero])
    for c in range(2):
        k.dma("sp", zC[c, :, 0:1], zero_t[:, 0:1], [B_zero], [k.buf("zc0")], slow=True)
        k.dma("sp", zC[c, :, CTX + 1:CTX + 2], zero_t[:, 0:1], [B_zero], [k.buf("zc1")], slow=True)
    with sbt("cc_t", [128, 8, 2], F32) as cc_t:
        B_cc = k.buf("cc")
        k.dma("sp", cc_t[:], cc, (), [B_cc])
        k.act(scb[:], cc_t[:], AF.Silu, [B_cc], [B_scb])
        k.barrier()

    def rstd_of(ssum_ap, P, T, inv_n, R, rt, rs):
        rt_ap, rt_b = rt.next()
        rs_ap, rs_b = rs.next()
        k.act(rt_ap[0:P, 0:T], ssum_ap, AF.Ln, R, [rt_b], bias=eps_col[0:P, :], scale=inv_n)
        k.act(rs_ap[0:P, 0:T], rt_ap[0:P, 0:T], AF.Exp, [rt_b], [rs_b], scale=-0.5)
        return rs_ap, rs_b

    def phase_tin():
        with ExitStack() as ph:
            xt = Ring(k, ph.enter_context(sbt("xt", [128, 2, 4, D], F32)), 2, "xt")
            ht = Ring(k, ph.enter_context(sbt("ht", [128, 2, 8, TB], F32)), 2, "ht")
            n = 0
            ada = AdaJob(0, ph)
            ada.dma(0)
            for bi_, (hd, T, col0, stream) in enumerate(blocks):
                src = x[col0:col0 + T, :] if stream == 0 else ctxin
                nt = T // 128
                xa, xb = xt.next()
                k.dma("sp", xa[:, 0:nt, :], src.rearrange("(j p) f -> p j f", p=128), (), [xb])
                ha, hb = ht.next()
                for c in range(8):
                    pa, pb = ps_next()
                    for j in range(nt):
                        k.op("pe", lambda e: e.transpose(pa[:, j * 128:(j + 1) * 128], xa[:, j, c * 128:(c + 1) * 128],
                                                         ident[:]), [xb, B_ident], [pb], signal=(j == nt - 1))
                    if n % 2 == 0:
                        k.copy(ha[:, c, 0:T], pa[:, 0:T], [pb], [hb])
                    else:
                        k.act(ha[:, c, 0:T], pa[:, 0:T], AF.Copy, [pb], [hb])
                n += 1
                hB = k.buf("hS")
                k.dma("sp", hd, ha[:, :, 0:T], [hb], [hB])
                ada.step(bi_)
            ada.finish()
            k.barrier()

    def phase_tout():
        with ExitStack() as ph:
            ht = Ring(k, ph.enter_context(sbt("ht", [128, 2, 8, TB], F32)), 2, "ht")
            ot = Ring(k, ph.enter_context(sbt("ot", [128, 2, 4, D], F32)), 2, "ot")
            oB = k.buf("out")
            n = 0
            for (hd, T, col0, stream) in blocks[:8]:
                ha, hb = ht.next()
                k.dma("sp", ha[:], hd, (), [hb])
                oa, ob = ot.next()
                for j in range(4):
                    for c2 in range(2):
                        pa, pb = ps_next()
                        for cc_ in range(4):
                            c = c2 * 4 + cc_
                            k.op("pe", lambda e: e.transpose(pa[:, cc_ * 128:(cc_ + 1) * 128],
                                                             ha[:, c, j * 128:(j + 1) * 128], ident[:]),
                                 [hb, B_ident], [pb], signal=(cc_ == 3))
                        if n % 2 == 0:
                            k.copy(oa[:, j, c2 * 512:(c2 + 1) * 512], pa[:], [pb], [ob])
                        else:
                            k.act(oa[:, j, c2 * 512:(c2 + 1) * 512], pa[:], AF.Copy, [pb], [ob])
                n += 1
                k.dma("sp", out[col0:col0 + T, :].rearrange("(j p) f -> p j f", p=128), oa[:], [ob], [oB])
            k.barrier()

    class AdaJob:
        def __init__(self, l, ph):
            self.l = l
            self.wa = Ring(k, ph.enter_context(sbt("wa", [128, 2, 8, D], BF16)), 2, "wa")
            self.bt = ph.enter_context(sbt("bt", [128, 72], F32))
            self.B_bt = k.buf("bt")
            k.dma("sp", self.bt[:], b_ada_t[l], (), [self.B_bt])
            k.dma("sp", vecs_l[l][:], vecs_d[l], (), [B_vecs_l[l]])
            ps_i[0] += 1
            while ps_i[0] % 8 in ps_reserved:
                ps_i[0] += 1
            self.j = ps_i[0] % 8
            ps_reserved.add(self.j)
            self.pa, self.pb = psum[:, self.j], PB[self.j]
            self.wv = w_ada[l].rearrange("(k p) n -> p k n", p=128)
            self.slots = {}

        def dma(self, ch):
            if ch < 9 and ch not in self.slots:
                wa_ap, wa_b = self.wa.next()
                k.dma("pool", wa_ap, self.wv[:, :, ch * D:(ch + 1) * D], (), [wa_b])
                self.slots[ch] = (wa_ap, wa_b)

        def step(self, ch):
            if ch >= 9:
                return
            self.dma(ch)
            wa_ap, wa_b = self.slots[ch]
            for fo in range(8):
                j = ch * 8 + fo
                for kk in range(8):
                    k.mm(self.pa[:, 2 * j:2 * j + 2], wa_ap[:, kk, fo * 128:(fo + 1) * 128], scb[:, kk, :],
                         kk == 0, kk == 7, [wa_b, B_scb], [self.pb])
            self.dma(ch + 1)

        def finish(self):
            l = self.l
            mv, dr, vc = modv_l[l], der_l[l], vecs_l[l]
            Bm, Bd, Bv = B_modv_l[l], B_der_l[l], B_vecs_l[l]
            pa, pb = self.pa, self.pb
            for s_ in range(2):
                k.tt(mv[:, s_, :], pa[:, 0:144].rearrange("p (j s) -> p j s", s=2)[:, :, s_], self.bt[:], ALU.add,
                     [pb, self.B_bt], [Bm])
            for s_ in range(2):
                for (ch_scale, vcol, ch_shift, ch_gate, gmul) in ((1, V_GF1, 0, 2, 0.5), (4, V_GMIX, 3, 5, 1.0),
                                                                  (7, V_GF2, 6, 8, 0.5)):
                    k.stt(dr[:, s_, ch_scale * 8:ch_scale * 8 + 8], mv[:, s_, ch_scale * 8:ch_scale * 8 + 8], 1.0,
                          vc[:, vcol:vcol + 8], ALU.add, ALU.mult, [Bm, Bv], [Bd])
                    k.copy(dr[:, s_, ch_shift * 8:ch_shift * 8 + 8], mv[:, s_, ch_shift * 8:ch_shift * 8 + 8], [Bm], [Bd])
                    k.ts(dr[:, s_, ch_gate * 8:ch_gate * 8 + 8], mv[:, s_, ch_gate * 8:ch_gate * 8 + 8], gmul, None,
                         ALU.mult, None, [Bm], [Bd])
            ps_reserved.discard(self.j)

    def norm_modulate(ph_rings, ha, hb, T, stream, ch_scale, ch_shift, xn_ap, xn_b):
        sq, rt, rs, tmp = ph_rings
        pa, pb = ps_next()
        for c in range(8):
            sq_ap, sq_b = sq.next()
            k.act(sq_ap[:, 0:T], ha[:, c, 0:T], AF.Square, [hb], [sq_b])
            k.mm(pa[:, 0:T], ones_b[:], sq_ap[:, 0:T], c == 0, c == 7, [sq_b, B_ones], [pb], sig=True)
        rs_ap, rs_b = rstd_of(pa[:, 0:T], 128, T, 1.0 / D, [pb], rt, rs)
        for c in range(8):
            t_ap, t_b = tmp.next()
            k.stt(t_ap[:, 0:T], ha[:, c, 0:T], der[:, stream, ch_scale * 8 + c:ch_scale * 8 + c + 1], rs_ap[:, 0:T],
                  ALU.mult, ALU.mult, [hb, rs_b, B_der], [t_b])
            k.act(xn_ap[:, c, 0:T], t_ap[:, 0:T], AF.Identity, [t_b, B_der], [xn_b],
                  bias=der[:, stream, ch_shift * 8 + c:ch_shift * 8 + c + 1])

    def phase_ffn(l, which, streams):
        ch_shift, ch_scale, ch_gate = (0, 1, 2) if which == 0 else (6, 7, 8)
        with ExitStack() as ph:
            win = ph.enter_context(sbt("win", [128, 8, 2 * DFF], BF16))
            wout = ph.enter_context(sbt("wout", [128, NJ, D], BF16))
            ht = Ring(k, ph.enter_context(sbt("ht", [128, 2, 8, TB], F32)), 2, "ht")
            sq = Ring(k, ph.enter_context(sbt("sq", [128, 2, TB], BF16)), 2, "sq")
            rt = Ring(k, ph.enter_context(sbt("rt", [128, 1, TB], F32)), 1, "rt")
            rs = Ring(k, ph.enter_context(sbt("rs", [128, 1, TB], F32)), 1, "rs")
            tmp = Ring(k, ph.enter_context(sbt("tmp", [128, 2, TB], F32)), 2, "tmp")
            sa = Ring(k, ph.enter_context(sbt("sa", [128, 2, TB], F32)), 2, "sa")
            xn = Ring(k, ph.enter_context(sbt("xn", [128, 2, 8, TB], BF16)), 2, "xn")
            g = ph.enter_context(sbt("g", [128, 6, TB], BF16)); B_g = k.buf("g")
            wiv = w_f_in[which][l].rearrange("(k p) n -> p k n", p=128)
            wov = w_f_out[which][l].rearrange("(j p) n -> p j n", p=128)
            B_win, B_wout = [], []
            for (j0, j1) in JGROUPS:
                bw = k.buf("win"); bo = k.buf("wout")
                k.dma("pool", win[:, :, j0 * 128:j1 * 128], wiv[:, :, j0 * 128:j1 * 128], (), [bw])
                k.dma("pool", win[:, :, DFF + j0 * 128:DFF + j1 * 128], wiv[:, :, DFF + j0 * 128:DFF + j1 * 128], (), [bw])
                k.dma("pool", wout[:, j0:j1, :], wov[:, j0:j1, :], (), [bo])
                B_win.append(bw); B_wout.append(bo)
            blks = [b for b in blocks if b[3] in streams]
            loaded = {}

            def load(i):
                if i < len(blks) and i not in loaded:
                    hd, T, col0, stream = blks[i]
                    ha, hb = ht.next()
                    k.dma("sp", ha[:, :, 0:T], hd, (), [hb])
                    loaded[i] = (ha, hb)
            load(0)
            xns = {}

            def prep(i):
                if i < len(blks):
                    hd_, T_, col0_, stream_ = blks[i]
                    ha_, hb_ = loaded[i]
                    xa_, xb_ = xn.next()
                    norm_modulate((sq, rt, rs, tmp), ha_, hb_, T_, stream_, ch_scale, ch_shift, xa_, xb_)
                    xns[i] = (xa_, xb_)
            prep(0)
            for bi, (hd, T, col0, stream) in enumerate(blks):
                ha, hb = loaded[bi]
                load(bi + 1)
                xn_ap, B_xn = xns[bi]
                for gi, (j0, j1) in enumerate(JGROUPS):
                    if gi == 2:
                        prep(bi + 1)
                    for j in range(j0, j1):
                        pa, pab = ps_next()
                        pb_, pbb = ps_next()
                        for kk in range(8):
                            k.mm(pa[:, 0:T], win[:, kk, j * 128:(j + 1) * 128], xn_ap[:, kk, 0:T], kk == 0, kk == 7,
                                 [B_win[gi], B_xn], [pab])
                        for kk in range(8):
                            k.mm(pb_[:, 0:T], win[:, kk, DFF + j * 128:DFF + (j + 1) * 128], xn_ap[:, kk, 0:T], kk == 0,
                                 kk == 7, [B_win[gi], B_xn], [pbb])
                        sa_ap, sa_b = sa.next()
                        k.act(sa_ap[:, 0:T], pa[:, 0:T], AF.Silu, [pab], [sa_b])
                        k.tt(g[:, j - j0, 0:T], sa_ap[:, 0:T], pb_[:, 0:T], ALU.mult, [sa_b, pbb], [B_g])
                    for c in range(8):
                        po, pob = ps_next()
                        for j in range(j0, j1):
                            k.mm(po[:, 0:T], wout[:, j, c * 128:(c + 1) * 128], g[:, j - j0, 0:T], j == j0, j == j1 - 1,
                                 [B_wout[gi], B_g], [pob])
                        k.stt(ha[:, c, 0:T], po[:, 0:T], der[:, stream, ch_gate * 8 + c:ch_gate * 8 + c + 1],
                              ha[:, c, 0:T], ALU.mult, ALU.add, [pob, B_der, hb], [hb])
                k.dma("sp", hd, ha[:, :, 0:T], [hb], [k.buf("hS")])
            k.barrier()

    def phase_mix(l, last):
        with ExitStack() as ph:
            wmi = ph.enter_context(sbt("wmi", [128, 8, MIXC], BF16)); B_wmi = k.buf("wmi")
            wqu = ph.enter_context(sbt("wqu", [128, 3, 2 * NH * QK], BF16)); B_wqu = k.buf("wqu")
            wkk = ph.enter_context(sbt("wkk", [128, 2, NH * QK], BF16)); B_wkk = k.buf("wkk")
            wkv = ph.enter_context(sbt("wkv", [128, 2, NH * 64], BF16)); B_wkv = k.buf("wkv")
            wsp = ph.enter_context(sbt("wsp", [128, 4, 128], BF16)); B_wsp = k.buf("wsp")
            bsp = ph.enter_context(sbt("bsp", [1, 512], F32)); B_bsp = k.buf("bsp")
            gsg = ph.enter_context(sbt("gsg", [128, 256], F32)); B_gsg = k.buf("gsg")
            grow = ph.enter_context(sbt("grow", [1, 2 * QK], F32)); B_grow = k.buf("grow")
            gmax = ph.enter_context(sbt("gmax", [1, 4], F32)); B_gmax = k.buf("gmax")
            k.dma("pool", wmi[:], w_mi[l].rearrange("(k p) n -> p k n", p=128), (), [B_wmi])
            k.dma("pool", wqu[:], w_qu[l].rearrange("(k p) n -> p k n", p=128), (), [B_wqu])
            k.dma("pool", wkk[:], w_kvk[l].rearrange("(k p) n -> p k n", p=128), (), [B_wkk])
            k.dma("pool", wkv[:], w_kvv[l].rearrange("(k p) n -> p k n", p=128), (), [B_wkv])
            k.dma("pool", wsp[:], w_spT[l].rearrange("g q p -> q g p"), (), [B_wsp])
            k.dma("sp", bsp[:], b_sp[l], (), [B_bsp])
            k.dma("sp", gsg[:], gsgu_d[l], (), [B_gsg])
            k.dma("sp", grow[:], grow_d[l], (), [B_grow])
            k.op("dve", lambda e: e.tensor_reduce(out=gmax[:, 0:1], in_=grow[:, 0:QK], axis=AX.X, op=ALU.max,
                                                  apply_absolute_value=True), [B_grow], [B_gmax])
            k.op("dve", lambda e: e.tensor_reduce(out=gmax[:, 1:2], in_=grow[:, QK:2 * QK], axis=AX.X, op=ALU.max,
                                                  apply_absolute_value=True), [B_grow], [B_gmax])
            k.stt(gmax[:, 2:3], gmax[:, 0:1], -float(np.sqrt(QK)), gmax[:, 1:2], ALU.mult, ALU.mult,
                  [B_gmax], [B_gmax])
            pa, pb = ps_next()
            k.mm(pa[:, 0:1], ones_f[:], gmax[:, 2:3], True, True, [B_onesf, B_gmax], [pb])
            k.copy(nshift[:], pa[:, 0:1], [pb], [B_nshift])

            ht = Ring(k, ph.enter_context(sbt("ht", [128, 2, 8, TB], F32)), 2, "ht")
            rp = Ring(k, ph.enter_context(sbt("rp", [QK, 2, 2, TB], F32)), 2, "rp")
            sq = Ring(k, ph.enter_context(sbt("sq", [128, 3, TB], BF16)), 3, "sq")
            rt = Ring(k, ph.enter_context(sbt("rt", [128, 2, TB], F32)), 2, "rt")
            rs = Ring(k, ph.enter_context(sbt("rs", [128, 2, TB], F32)), 2, "rs")
            tmp = Ring(k, ph.enter_context(sbt("tmp", [128, 2, TB], F32)), 2, "tmp")
            t1 = Ring(k, ph.enter_context(sbt("t1", [QK, 2, TB], F32)), 2, "t1")
            t2 = Ring(k, ph.enter_context(sbt("t2", [QK, 2, TB], F32)), 2, "t2")
            t2k = ph.enter_context(sbt("t2k", [QK, TB], F32)); B_t2k = k.buf("t2k")
            qo = Ring(k, ph.enter_context(sbt("qo", [QK, 3, TB], BF16)), 3, "qo")
            hn = ph.enter_context(sbt("hn", [128, 8, TB], BF16)); B_hn = k.buf("hn")
            qn = ph.enter_context(sbt("qn", [128, 3, TB], BF16)); B_qn = k.buf("qn")
            kvn = ph.enter_context(sbt("kvn", [128, 2, TB], BF16)); B_kvn = k.buf("kvn")
            kpe = ph.enter_context(sbt("kpe", [32, TB], BF16)); B_kpe = k.buf("kpe")
            vt = Ring(k, ph.enter_context(sbt("vt", [128, 2, NH, 4, 64], BF16)), 2, "vt")
            gu = ph.enter_context(sbt("gu", [64, 4, TB], F32)); B_gu = k.buf("gu")
            gv = Ring(k, ph.enter_context(sbt("gv", [128, 2, 256], F32)), 2, "gv")
            gv2 = Ring(k, ph.enter_context(sbt("gv2", [128, 2, 256], F32)), 2, "gv2")
            vst = Ring(k, ph.enter_context(sbt("vst", [128, 2, 8], F32)), 2, "vst")
            vnb = Ring(k, ph.enter_context(sbt("vnb", [128, 2, 256], BF16)), 2, "vnb")
            sgt = ph.enter_context(sbt("sgt", [64, 4, TB], F32)); B_sgt = k.buf("sgt")
            ysg = Ring(k, ph.enter_context(sbt("ysg", [64, 2, 4, TB], BF16)), 2, "ysg")
            xi = Ring(k, ph.enter_context(sbt("xi", [128, 2, TB], F32)), 2, "xi")
            zt = Ring(k, ph.enter_context(sbt("zt", [128, 2, 2, TB], F32)), 2, "zt")
            bgt = Ring(k, ph.enter_context(sbt("bgt", [128, 2, 2, TB], F32)), 2, "bgt")
            B_qS = k.buf("qS"); B_kI = k.buf("kI"); B_vI = k.buf("vI"); B_sgS = k.buf("sgS")
            B_zS = k.buf("zS"); B_bgS = k.buf("bgS")

            def colsl(c0, n):
                return slice(c0, c0 + n)
            loaded = {}

            def load(i):
                if i < len(blocks) and i not in loaded:
                    hd, T, col0, stream = blocks[i]
                    ha, hb = ht.next()
                    k.dma("sp", ha[:, :, 0:T], hd, (), [hb])
                    ra, rb = rp.next()
                    k.dma("sp", ra[:, :, 0:T], rope[:, :, col0:col0 + T].rearrange("a d t -> d a t"), (), [rb])
                    loaded[i] = (ha, hb, ra, rb)
            load(0)
            for bi, (hd, T, col0, stream) in enumerate(blocks):
                ha, hb, ra, rb = loaded[bi]
                load(bi + 1)
                isctx = stream == 1
                full = not (isctx and last)
                norm_modulate((sq, rt, rs, tmp), ha, hb, T, stream, 4, 3, hn, B_hn)

                def proj(out_ap, out_b, c0, M):
                    for kk in range(8):
                        k.mm(out_ap, wmi[:, kk, c0:c0 + M], hn[:, kk, 0:T], kk == 0, kk == 7, [B_wmi, B_hn], [out_b])

                def latent_norm(c0, nch, gcol, dst, dst_b):
                    pl = [ps_next() for _ in range(nch)]
                    for c in range(nch):
                        proj(pl[c][0][:, 0:T], pl[c][1], c0 + c * 128, 128)
                    pss, pssb = ps_next()
                    for c in range(nch):
                        sq_ap, sq_b = sq.next()
                        k.act(sq_ap[:, 0:T], pl[c][0][:, 0:T], AF.Square, [pl[c][1]], [sq_b])
                        k.mm(pss[:, 0:T], ones_b[:], sq_ap[:, 0:T], c == 0, c == nch - 1, [sq_b, B_ones], [pssb], sig=True)
                    rs_ap, rs_b = rstd_of(pss[:, 0:T], 128, T, 1.0 / (nch * 128), [pssb], rt, rs)
                    for c in range(nch):
                        k.stt(dst[:, c, 0:T], pl[c][0][:, 0:T], vecs[:, gcol + c:gcol + c + 1], rs_ap[:, 0:T], ALU.mult,
                              ALU.mult, [pl[c][1], rs_b, B_vecs], [dst_b])

                def head_finish(pA, pAb, t2_ap, t2_b, gcol, dst_dram, dst_B):
                    sq_ap, sq_b = sq.next()
                    k.act(sq_ap[0:QK, 0:T], pA[0:QK, 0:T], AF.Square, [pAb], [sq_b])
                    pss, pssb = ps_next()
                    k.mm(pss[0:QK, 0:T], ones_b[0:QK, 0:QK], sq_ap[0:QK, 0:T], True, True, [sq_b, B_ones], [pssb])
                    rs_ap, rs_b = rstd_of(pss[0:QK, 0:T], QK, T, 1.0 / QK, [pssb], rt, rs)
                    t1_ap, t1_b = t1.next()
                    k.stt(t1_ap[:, 0:T], pA[0:QK, 0:T], vecs[0:QK, gcol:gcol + 1], ra[:, 0, 0:T], ALU.mult, ALU.mult,
                          [pAb, B_vecs, rb], [t1_b])
                    k.tt(t1_ap[:, 0:T], t1_ap[:, 0:T], t2_ap, ALU.add, [t1_b, t2_b], [t1_b], en="pool")
                    q_ap, q_b = qo.next()
                    k.tt(q_ap[:, 0:T], t1_ap[:, 0:T], rs_ap[0:QK, 0:T], ALU.mult, [t1_b, rs_b], [q_b])
                    k.dma("sp", dst_dram, q_ap[:, 0:T], [q_b], [dst_B])

                if full:
                    latent_norm(0, 3, V_GQL, qn, B_qn)
                latent_norm(384, 2, V_GKV, kvn, B_kvn)
                pk, pkb = ps_next()
                proj(pk[0:32, 0:T], pkb, 640, 32)
                k.act(kpe[:, 0:T], pk[0:32, 0:T], AF.Copy, [pkb], [B_kpe])
                pB, pBb = ps_next()
                k.mm(pB[0:QK, 0:T], sel[:, QK:2 * QK], kpe[:, 0:T], True, True, [B_sel, B_kpe], [pBb])
                k.stt(t2k[:, 0:T], pB[0:QK, 0:T], vecs[0:QK, V_GKHP:V_GKHP + 1], ra[:, 1, 0:T], ALU.mult, ALU.mult,
                      [pBb, B_vecs, rb], [B_t2k])
                jobs = ([("q", h) for h in range(NH)] if full else []) + [("k", h) for h in range(NH)]
                st = {}

                def stage1(ji):
                    kind, h = jobs[ji]
                    pA, pAb = ps_next()
                    d = dict(pA=pA, pAb=pAb)
                    if kind == "q":
                        pB2, pB2b = ps_next()
                        for c in range(3):
                            k.mm(pA[0:QK, 0:T], wqu[:, c, h * QK:(h + 1) * QK], qn[:, c, 0:T], c == 0, c == 2,
                                 [B_wqu, B_qn], [pAb])
                        for c in range(3):
                            k.mm(pB2[0:QK, 0:T], wqu[:, c, NH * QK + h * QK:NH * QK + (h + 1) * QK], qn[:, c, 0:T], c == 0,
                                 c == 2, [B_wqu, B_qn], [pB2b])
                    else:
                        for c in range(2):
                            k.mm(pA[0:QK, 0:T], wkk[:, c, h * QK:(h + 1) * QK], kvn[:, c, 0:T], c == 0, False,
                                 [B_wkk, B_kvn], [pAb])
                        k.mm(pA[0:QK, 0:T], sel[:, 0:QK], kpe[:, 0:T], False, True, [B_sel, B_kpe], [pAb])
                    sq_ap, sq_b = sq.next()
                    k.act(sq_ap[0:QK, 0:T], pA[0:QK, 0:T], AF.Square, [pAb], [sq_b])
                    d["sq"] = (sq_ap, sq_b)
                    if kind == "q":
                        t2_ap, t2_b = t2.next()
                        k.stt(t2_ap[:, 0:T], pB2[0:QK, 0:T], vecs[0:QK, V_GQHP:V_GQHP + 1], ra[:, 1, 0:T], ALU.mult,
                              ALU.mult, [pB2b, B_vecs, rb], [t2_b])
                        d["t2"] = (t2_ap[:, 0:T], t2_b)
                    else:
                        d["t2"] = (t2k[:, 0:T], B_t2k)
                    st[ji] = d

                def stage2(ji):
                    kind, h = jobs[ji]
                    d = st.pop(ji)
                    pA, pAb = d["pA"], d["pAb"]
                    sq_ap, sq_b = d["sq"]
                    pss, pssb = ps_next()
                    k.mm(pss[0:QK, 0:T], ones_b[0:QK, 0:QK], sq_ap[0:QK, 0:T], True, True, [sq_b, B_ones], [pssb])
                    gcol = V_GQH if kind == "q" else V_GKH
                    t1_ap, t1_b = t1.next()
                    k.stt(t1_ap[:, 0:T], pA[0:QK, 0:T], vecs[0:QK, gcol:gcol + 1], ra[:, 0, 0:T], ALU.mult, ALU.mult,
                          [pAb, B_vecs, rb], [t1_b])
                    k.tt(t1_ap[:, 0:T], t1_ap[:, 0:T], d["t2"][0], ALU.add, [t1_b, d["t2"][1]], [t1_b])
                    rs_ap, rs_b = rstd_of(pss[0:QK, 0:T], QK, T, 1.0 / QK, [pssb], rt, rs)
                    q_ap, q_b = qo.next()
                    k.tt(q_ap[:, 0:T], t1_ap[:, 0:T], rs_ap[0:QK, 0:T], ALU.mult, [t1_b, rs_b], [q_b], en="pool")
                    if kind == "q":
                        k.dma("sp", qS[h, :, col0:col0 + T], q_ap[:, 0:T], [q_b], [B_qS])
                    else:
                        k.dma("sp", kI[h // 2][(h % 2) * QK:(h % 2 + 1) * QK, col0:col0 + T], q_ap[:, 0:T], [q_b], [B_kI])

                if full:
                    for ji in range(len(jobs) + 1):
                        if ji < len(jobs):
                            stage1(ji)
                        if ji >= 1:
                            stage2(ji - 1)
                else:
                    for h in range(NH):
                        pA, pAb = ps_next()
                        for c in range(2):
                            k.mm(pA[0:QK, 0:T], wkk[:, c, h * QK:(h + 1) * QK], kvn[:, c, 0:T], c == 0, False,
                                 [B_wkk, B_kvn], [pAb])
                        k.mm(pA[0:QK, 0:T], sel[:, 0:QK], kpe[:, 0:T], False, True, [B_sel, B_kpe], [pAb])
                        head_finish(pA, pAb, t2k[:, 0:T], B_t2k, V_GKH,
                                    kI[h // 2][(h % 2) * QK:(h % 2 + 1) * QK, col0:col0 + T], B_kI)
                va, vb = vt.next()
                for j in range(T // 128):
                    pv, pvb = ps_next()
                    for c in range(2):
                        k.mm(pv[:, :], kvn[:, c, j * 128:(j + 1) * 128], wkv[:, c, :], c == 0, c == 1, [B_kvn, B_wkv], [pvb])
                    k.act(va[:, :, j, :], pv[:, :].rearrange("p (h d) -> p h d", d=64), AF.Copy, [pvb], [vb])
                for c4 in range(4):
                    k.dma("sp", vI[c4].rearrange("(h p) (t d) -> p h t d", p=128, d=64)[:, :, col0 // 128:(col0 + T) // 128, :],
                          va[:, 2 * c4:2 * c4 + 2, 0:T // 128, :], [vb], [B_vI])
                if not full:
                    continue
                for gq in range(4):
                    pu, pub = ps_next()
                    proj(pu[0:64, 0:T], pub, 672 + gq * 64, 64)
                    k.act(gu[:, gq, 0:T], pu[0:64, 0:T], AF.Gelu_apprx_tanh, [pub], [B_gu])
                psg = [ps_next() for _ in range(4)]
                for j in range(T // 128):
                    pv, pvb = ps_next()
                    for kk in range(8):
                        k.mm(pv[:, 0:256], hn[:, kk, j * 128:(j + 1) * 128], wmi[:, kk, 928:1184], kk == 0, kk == 7,
                             [B_hn, B_wmi], [pvb])
                    gv_ap, gv_b = gv.next()
                    k.act(gv_ap[:], pv[:, 0:256], AF.Gelu_apprx_tanh, [pvb], [gv_b])
                    g2_ap, g2_b = gv2.next()
                    k.tt(g2_ap[:], gv_ap[:], gv_ap[:], ALU.mult, [gv_b], [g2_b])
                    st_ap, st_b = vst.next()
                    k.op("dve", lambda e: e.tensor_reduce(out=st_ap[:, 0:4], in_=g2_ap[:].rearrange("p (g d) -> p g d", d=64),
                                                          axis=AX.X, op=ALU.add), [g2_b], [st_b])
                    k.act(st_ap[:, 4:8], st_ap[:, 0:4], AF.Sqrt, [st_b], [st_b], bias=eps_col[:, :], scale=1.0 / 64)
                    k.recip(st_ap[:, 0:4], st_ap[:, 4:8], [st_b], [st_b])
                    k.tt(g2_ap[:].rearrange("p (g d) -> p g d", d=64), gv_ap[:].rearrange("p (g d) -> p g d", d=64),
                         st_ap[:, 0:4].unsqueeze(2).to_broadcast([128, 4, 64]), ALU.mult, [gv_b, st_b], [g2_b])
                    vn_ap, vn_b = vnb.next()
                    k.tt(vn_ap[:], g2_ap[:], gsg[:], ALU.mult, [g2_b, B_gsg], [vn_b])
                    for gq in range(4):
                        k.mm(psg[gq][0][0:64, j * 128:(j + 1) * 128], vn_ap[:, gq * 64:(gq + 1) * 64], wsp[:, gq, :], True,
                             False, [vn_b, B_wsp], [psg[gq][1]])
                        k.mm(psg[gq][0][0:64, j * 128:(j + 1) * 128], ones_f[0:1, 0:64], bsp[:, gq * 128:(gq + 1) * 128],
                             False, True, [B_onesf, B_bsp], [psg[gq][1]])
                for gq in range(4):
                    k.tt(sgt[:, gq, 0:T], psg[gq][0][0:64, 0:T], gu[:, gq, 0:T], ALU.mult, [psg[gq][1], B_gu], [B_sgt])
                pss, pssb = ps_next()
                for gq in range(4):
                    sq_ap, sq_b = sq.next()
                    k.act(sq_ap[0:64, 0:T], sgt[:, gq, 0:T], AF.Square, [B_sgt], [sq_b])
                    k.mm(pss[0:64, 0:T], ones_b[0:64, 0:64], sq_ap[0:64, 0:T], gq == 0, gq == 3, [sq_b, B_ones], [pssb], sig=True)
                rs_ap, rs_b = rstd_of(pss[0:64, 0:T], 64, T, 1.0 / 256, [pssb], rt, rs)
                ys_ap, ys_b = ysg.next()
                for gq in range(4):
                    k.stt(ys_ap[:, gq, 0:T], sgt[:, gq, 0:T], vecs[0:64, V_GOS + gq:V_GOS + gq + 1], rs_ap[0:64, 0:T],
                          ALU.mult, ALU.mult, [B_sgt, B_vecs, rs_b], [ys_b])
                k.dma("sp", sgS[:, :, col0:col0 + T].rearrange("g d t -> d g t"), ys_ap[:, :, 0:T], [ys_b], [B_sgS])
                z_ap, z_b = zt.next()
                bg_ap, bg_b = bgt.next()
                for c in range(2):
                    pc, pcb = ps_next()
                    px, pxb = ps_next()
                    proj(pc[:, 0:T], pcb, 1440 + c * 128, 128)
                    proj(px[:, 0:T], pxb, 1696 + c * 128, 128)
                    xi_ap, xi_b = xi.next()
                    k.act(xi_ap[:, 0:T], px[:, 0:T], AF.Copy, [pxb], [xi_b])
                    k.tt(z_ap[:, c, 0:T], pc[:, 0:T], xi_ap[:, 0:T], ALU.mult, [pcb, xi_b], [z_b])
                    pg, pgb = ps_next()
                    proj(pg[:, 0:T], pgb, 1184 + c * 128, 128)
                    k.act(bg_ap[:, c, 0:T], pg[:, 0:T], AF.Copy, [pgb], [bg_b])
                if isctx:
                    k.dma("sp", zC[:, :, 1:1 + T].rearrange("c p t -> p c t"), z_ap[:, :, 0:T], [z_b], [B_zS])
                else:
                    k.dma("sp", zS[:, :, 1 + col0:1 + col0 + T].rearrange("c p t -> p c t"), z_ap[:, :, 0:T], [z_b], [B_zS])
                k.dma("sp", bgS[:, :, col0:col0 + T].rearrange("c p t -> p c t"), bg_ap[:, :, 0:T], [bg_b], [B_bgS])
            k.barrier()

    def phase_xchg():
        rg = [[2 * i, 2 * i + 1] for i in range(ncores // 2)]
        e = k.engs["pool"]
        B_zhI = k.buf("zhI")
        for c in range(2):
            k.dma("pool", zhI[0:1, c * 128:(c + 1) * 128].rearrange("a p -> p a"), zS[c, :, 1:2], (), [B_zhI], slow=True)
            k.dma("pool", zhI[1:2, c * 128:(c + 1) * 128].rearrange("a p -> p a"), zS[c, :, SEQ_C:SEQ_C + 1], (), [B_zhI], slow=True)
        k.deps(e, [B_zhI], [])
        B_zhO = k.buf("P:zhO")
        for c in range(4):
            xb["k"][c] = k.buf("P:kO")
            xb["v"][c] = k.buf("P:vO")
        order = [(zhI, zhO, B_zhO)]
        for c in range(4):
            order += [(kI[c], kO[c], xb["k"][c]), (vI[c], vO[c], xb["v"][c])]
        for (ci, co, bb) in order:
            sc_ = k.new_sem("cc")
            e.eng.collective_compute("AllGather", ALU.bypass, replica_groups=rg, ins=[ci], outs=[co]).then_inc(sc_.h, 1)
            sc_.count = 1
            bb.w = (sc_, 1)
        for c in range(2):
            k.dma("pool", zh[:, c, 0:1], zhO[1:2, c * 128:(c + 1) * 128].rearrange("a p -> p a"), [B_zhO], [B_zh], slow=True)
            k.dma("pool", zh[:, c, 1:2], zhO[2:3, c * 128:(c + 1) * 128].rearrange("a p -> p a"), [B_zhO], [B_zh], slow=True)
        for c in range(2):
            k.tt(zh[:, c, :], zh[:, c, :], hm[:, :], ALU.mult, [B_zh, B_hm], [B_zh])
        B_zS2 = k.buf("P:zS2")
        for c in range(2):
            k.dma("pool", zS[c, :, 0:1], zh[:, c, 0:1], [B_zh], [B_zS2], slow=True)
            k.dma("pool", zS[c, :, SEQ_C + 1:SEQ_C + 2], zh[:, c, 1:2], [B_zh], [B_zS2], slow=True)

    def phase_attn(l, last):
        scale = float(QK) ** -0.5
        with ExitStack() as ph:
            kT = Ring(k, ph.enter_context(sbt("kT", [QK, 2, 2 * SEQ_C + CTX], BF16)), 2, "kT")
            va = ph.enter_context(sbt("va", [128, 2, 66, 128], BF16))
            B_va = [k.buf("va0"), k.buf("va1")]
            B_vones = k.buf("vones")
            NQ = 4
            qT = Ring(k, ph.enter_context(sbt("qT", [QK, NQ, TB], BF16)), NQ, "qT")
            pT = Ring(k, ph.enter_context(sbt("pT", [128, 3, 3, TB], BF16)), 3, "pT")
            den = Ring(k, ph.enter_context(sbt("den", [64, 2, TB], F32)), 2, "den")
            ao = Ring(k, ph.enter_context(sbt("ao", [64, 2, TB], F32)), 2, "ao")
            B_aS = k.buf("aS")
            k.memset(va[:, :, :, 64:128], 1.0, [B_vones])
            S_B = [k.buf("S0"), k.buf("S1")]
            O_B = [k.buf("O0"), k.buf("O1")]
            for b_ in S_B + O_B:
                b_.excl = True
            lat_keys = [(t * 128, t) for t in range(34)] + [(NTOK + t * 128, 34 + t) for t in range(32)]
            ctx_keys = [(SEQ_C + t * 128, 32 + t) for t in range(2)]
            tasks = []
            for h in range(NH):
                for (hd, T, col0, stream) in blocks:
                    if stream == 1 and last:
                        continue
                    tasks.append(dict(h=h, T=T, col0=col0, keys=lat_keys if stream == 0 else ctx_keys))
            items = []
            for ti, t in enumerate(tasks):
                groups = [t["keys"][i:i + 3] for i in range(0, len(t["keys"]), 3)]
                for gi, gk in enumerate(groups):
                    items.append((ti, gi, len(groups), gk))
            heads = {}

            def load_head(h):
                if h < NH and h not in heads:
                    kt_ap, kt_b = kT.next()
                    vi = h % 2
                    hh = h % 2
                    k.dma("sp", kt_ap[:, 0:NTOK], kO[h // 2][hh * QK:(hh + 1) * QK, :], [xb["k"][h // 2]], [kt_b])
                    k.dma("sp", kt_ap[:, NTOK:NTOK + SEQ_C], kO[h // 2][2 * QK + hh * QK:2 * QK + (hh + 1) * QK, 0:SEQ_C], [xb["k"][h // 2]], [kt_b])
                    for (r0, t0, nt) in ((0, 0, 17), (0, 17, 17), (1, 0, 16), (1, 16, 16)):
                        k.dma("pool", va[:, vi, 34 * r0 + t0:34 * r0 + t0 + nt, 0:64],
                              vO[h // 2][r0 * 256 + (h % 2) * 128:r0 * 256 + (h % 2 + 1) * 128, t0 * 64:(t0 + nt) * 64]
                              .rearrange("p (t d) -> p t d", d=64), [xb["v"][h // 2]], [B_va[vi]])
                    heads[h] = (kt_ap, kt_b, vi)
            ql = {}

            def load_q(ti):
                if ti < len(tasks) and ti not in ql:
                    t = tasks[ti]
                    q_ap, q_b = qT.next()
                    k.dma("sp", q_ap[:, 0:t["T"]], qS[t["h"], :, t["col0"]:t["col0"] + t["T"]], (), [q_b])
                    ql[ti] = (q_ap, q_b)

            def prep_task(ti):
                if ti < len(tasks):
                    h = tasks[ti]["h"]
                    if ti == 0:
                        load_head(h)
                    for d in range(NQ - 1):
                        load_q(ti + d)

            def emit_S(idx):
                ti, gi, ng, gk = items[idx]
                t = tasks[ti]
                if gi == 0:
                    prep_task(ti)
                kt_ap, kt_b, vi = heads[t["h"]]
                q_ap, q_b = ql[ti]
                sslot = idx % 2
                for tj, (kc, vtile) in enumerate(gk):
                    k.mm(psum[:, 3 * sslot + tj, 0:t["T"]], kt_ap[:, kc:kc + 128], q_ap[:, 0:t["T"]], True, True,
                         [kt_b, q_b], [S_B[sslot]], sig=(tj == len(gk) - 1))

            def emit_E_PV(idx):
                ti, gi, ng, gk = items[idx]
                t = tasks[ti]
                T = t["T"]
                kt_ap, kt_b, vi = heads[t["h"]]
                sslot = idx % 2
                n = len(gk)
                p_ap, p_b = pT.next()
                k.act(p_ap[:, 0:n, 0:T], psum[:, 3 * sslot:3 * sslot + n, 0:T], AF.Exp, [S_B[sslot], B_nshift], [p_b],
                      bias=nshift[:, :], scale=scale)
                o = ti % 2
                po, pob = psum[:, 6 + o], O_B[o]
                if gi == 0 and (ti == 0 or tasks[ti - 1]["h"] != t["h"]):
                    load_head(t["h"] + 1)
                for tj, (kc, vtile) in enumerate(gk):
                    first = gi == 0 and tj == 0
                    lastmm = gi == ng - 1 and tj == n - 1
                    k.mm(po[:, 0:T], va[:, vi, vtile, :], p_ap[:, tj, 0:T], first, lastmm,
                         [B_va[vi], B_vones, p_b], [pob])
                if gi == ng - 1:
                    d_ap, d_b = den.next()
                    k.copy(d_ap[:, 0:T], po[64:128, 0:T], [pob], [d_b])
                    k.recip(d_ap[:, 0:T], d_ap[:, 0:T], [d_b], [d_b])
                    a_ap, a_b = ao.next()
                    k.tt(a_ap[:, 0:T], po[0:64, 0:T], d_ap[:, 0:T], ALU.mult, [pob, d_b], [a_b])
                    k.dma("sp", aS[t["h"], :, t["col0"]:t["col0"] + T], a_ap[:, 0:T], [a_b], [B_aS])

            emit_S(0)
            for idx in range(len(items)):
                if idx + 1 < len(items):
                    emit_S(idx + 1)
                emit_E_PV(idx)
            k.barrier()

    def phase_merge(l, last):
        with ExitStack() as ph:
            wA = ph.enter_context(sbt("wA", [64, NH, D], BF16)); B_wA = k.buf("wA")
            wS = ph.enter_context(sbt("wS", [64, 4, D], BF16)); B_wS = k.buf("wS")
            wC = ph.enter_context(sbt("wC", [128, 2, D], BF16)); B_wC = k.buf("wC")
            k.dma("pool", wA[:], w_mo[l, 0:512, :].rearrange("(h p) n -> p h n", p=64), (), [B_wA])
            k.dma("pool", wS[:], w_mo[l, 512:768, :].rearrange("(h p) n -> p h n", p=64), (), [B_wS])
            k.dma("pool", wC[:], w_mo[l, 768:1024, :].rearrange("(h p) n -> p h n", p=128), (), [B_wC])
            nht = 3 if last else 2
            ht = Ring(k, ph.enter_context(sbt("ht", [128, nht, 8, TB], F32)), nht, "ht")
            at = Ring(k, ph.enter_context(sbt("at", [64, 2, NH, TB], F32)), 2, "at")
            ys = Ring(k, ph.enter_context(sbt("ys", [64, 2, 4, TB], BF16)), 2, "ys")
            zt = Ring(k, ph.enter_context(sbt("zt", [128, 2, 2, TB + 2], F32)), 2, "zt")
            bgt = Ring(k, ph.enter_context(sbt("bgt", [128, 2, 2, TB], F32)), 2, "bgt")
            sq = Ring(k, ph.enter_context(sbt("sq", [128, 2, TB], BF16)), 2, "sq")
            rt = Ring(k, ph.enter_context(sbt("rt", [128, 2, TB], F32)), 2, "rt")
            rs = Ring(k, ph.enter_context(sbt("rs", [128, 2, TB], F32)), 2, "rs")
            ya = Ring(k, ph.enter_context(sbt("ya", [64, 2, NH, TB], BF16)), 2, "ya")
            cvt = ph.enter_context(sbt("cvt", [128, 2, TB], F32)); B_cvt = k.buf("cvt")
            yc = Ring(k, ph.enter_context(sbt("yc", [128, 2, 2, TB], BF16)), 2, "yc")
            blks = [b for b in blocks if not (b[3] == 1 and last)]
            loaded = {}

            def load(i):
                if i < len(blks) and i not in loaded:
                    hd, T, col0, stream = blks[i]
                    ha, hb = ht.next()
                    k.dma("sp", ha[:, :, 0:T], hd, (), [hb])
                    a_ap, a_b = at.next()
                    k.dma("pool", a_ap[:, :, 0:T], aS[:, :, col0:col0 + T].rearrange("h d t -> d h t"), (), [a_b])
                    y_ap, y_b = ys.next()
                    k.dma("pool", y_ap[:, :, 0:T], sgS[:, :, col0:col0 + T].rearrange("g d t -> d g t"), (), [y_b])
                    z_ap, z_b = zt.next()
                    if stream == 1:
                        k.dma("pool", z_ap[:, :, 0:T + 2], zC.rearrange("c p t -> p c t"), (), [z_b])
                    else:
                        k.dma("pool", z_ap[:, :, 0:T + 2], zS[:, :, col0:col0 + T + 2].rearrange("c p t -> p c t"), (), [z_b])
                    g_ap, g_b = bgt.next()
                    k.dma("pool", g_ap[:, :, 0:T], bgS[:, :, col0:col0 + T].rearrange("c p t -> p c t"), (), [g_b])
                    loaded[i] = (ha, hb, a_ap, a_b, y_ap, y_b, z_ap, z_b, g_ap, g_b)
            load(0)

            def prep(bi):
                hd, T, col0, stream = blks[bi]
                ha, hb, a_ap, a_b, y_ap, y_b, z_ap, z_b, g_ap, g_b = loaded[bi]
                ya_ap, ya_b = ya.next()
                yc_ap, yc_b = yc.next()
                pss, pssb = ps_next()
                for h in range(NH):
                    sq_ap, sq_b = sq.next()
                    k.act(sq_ap[0:64, 0:T], a_ap[:, h, 0:T], AF.Square, [a_b], [sq_b])
                    k.mm(pss[0:64, 0:T], ones_b[0:64, 0:64], sq_ap[0:64, 0:T], h == 0, h == NH - 1, [sq_b, B_ones], [pssb], sig=True)
                rs_ap, rs_b = rstd_of(pss[0:64, 0:T], 64, T, 1.0 / 512, [pssb], rt, rs)
                for h in range(NH):
                    k.stt(ya_ap[:, h, 0:T], a_ap[:, h, 0:T], vecs[0:64, V_GOA + h:V_GOA + h + 1], rs_ap[0:64, 0:T], ALU.mult,
                          ALU.mult, [a_b, B_vecs, rs_b], [ya_b])
                for c in range(2):
                    wc = V_WCV + c * 3
                    k.ts(cvt[:, c, 0:T], z_ap[:, c, 1:T + 1], vecs[:, wc + 1:wc + 2], None, ALU.mult, None,
                         [z_b, B_vecs], [B_cvt])
                    k.stt(cvt[:, c, 0:T], z_ap[:, c, 0:T], vecs[:, wc:wc + 1], cvt[:, c, 0:T], ALU.mult, ALU.add,
                          [z_b, B_vecs, B_cvt], [B_cvt])
                    k.stt(cvt[:, c, 0:T], z_ap[:, c, 2:T + 2], vecs[:, wc + 2:wc + 3], cvt[:, c, 0:T], ALU.mult, ALU.add,
                          [z_b, B_vecs, B_cvt], [B_cvt])
                    k.tt(cvt[:, c, 0:T], cvt[:, c, 0:T], g_ap[:, c, 0:T], ALU.mult, [B_cvt, g_b], [B_cvt], en="pool")
                pss, pssb = ps_next()
                for c in range(2):
                    sq_ap, sq_b = sq.next()
                    k.act(sq_ap[:, 0:T], cvt[:, c, 0:T], AF.Square, [B_cvt], [sq_b])
                    k.mm(pss[:, 0:T], ones_b[:], sq_ap[:, 0:T], c == 0, c == 1, [sq_b, B_ones], [pssb], sig=True)
                rs_ap, rs_b = rstd_of(pss[:, 0:T], 128, T, 1.0 / 256, [pssb], rt, rs)
                for c in range(2):
                    k.stt(yc_ap[:, c, 0:T], cvt[:, c, 0:T], vecs[:, V_GOC + c:V_GOC + c + 1], rs_ap[:, 0:T], ALU.mult, ALU.mult,
                          [B_cvt, B_vecs, rs_b], [yc_b])
                preps[bi] = (ya_ap, ya_b, yc_ap, yc_b)

            def main_mm(bi, ocs):
                hd, T, col0, stream = blks[bi]
                y_ap, y_b = loaded[bi][4], loaded[bi][5]
                ya_ap, ya_b, yc_ap, yc_b = preps[bi]
                res = []
                for oc in ocs:
                    po, pob = ps_next()
                    osl = slice(oc * 128, (oc + 1) * 128)
                    for h in range(NH):
                        k.mm(po[:, 0:T], wA[:, h, osl], ya_ap[:, h, 0:T], h == 0, False, [B_wA, ya_b], [pob])
                    for gq in range(4):
                        k.mm(po[:, 0:T], wS[:, gq, osl], y_ap[:, gq, 0:T], False, False, [B_wS, y_b], [pob])
                    for c in range(2):
                        k.mm(po[:, 0:T], wC[:, c, osl], yc_ap[:, c, 0:T], False, c == 1, [B_wC, yc_b], [pob])
                    res.append((oc, po, pob))
                return res

            def main_upd(bi, res):
                hd, T, col0, stream = blks[bi]
                ha, hb = loaded[bi][0], loaded[bi][1]
                for (oc, po, pob) in res:
                    k.stt(ha[:, oc, 0:T], po[:, 0:T], der[:, stream, 5 * 8 + oc:5 * 8 + oc + 1], ha[:, oc, 0:T], ALU.mult,
                          ALU.add, [pob, B_der, hb], [hb])

            preps = {}
            ada = None
            if not last:
                ada = AdaJob(l + 1, ph)
                ada.dma(0)
            prep(0)
            for bi, (hd, T, col0, stream) in enumerate(blks):
                load(bi + 1)
                if ada is not None:
                    ada.step(bi)
                ra_ = main_mm(bi, range(0, 4))
                if bi + 1 < len(blks):
                    prep(bi + 1)
                main_upd(bi, ra_)
                rb_ = main_mm(bi, range(4, 8))
                main_upd(bi, rb_)
                k.dma("sp", hd, loaded[bi][0][:, :, 0:T], [loaded[bi][1]], [k.buf("hS")])
            if ada is not None:
                ada.finish()
            k.barrier()

    eps_col = sb("eps_col", [128, 1]); B_eps = k.buf("P:eps")
    k.memset(eps_col[:], EPS, [B_eps])
    k.barrier()

    plan = []
    plan.append(("tin", phase_tin))
    for l in range(DEPTH):
        last = l == DEPTH - 1
        plan.append((f"set{l}", lambda l=l: set_layer(l)))
        plan.append((f"f1_{l}", lambda l=l: phase_ffn(l, 0, (0, 1))))
        plan.append((f"mix{l}", lambda l=l, last=last: phase_mix(l, last)))
        plan.append((f"x{l}", phase_xchg))
        plan.append((f"att{l}", lambda l=l, last=last: phase_attn(l, last)))
        plan.append((f"mrg{l}", lambda l=l, last=last: phase_merge(l, last)))
        plan.append((f"f2_{l}", lambda l=l, last=last: phase_ffn(l, 1, (0,) if last else (0, 1))))
    plan.append(("tout", phase_tout))
    for name, fn in plan:
        fn()
        if stop_after == name:
            break
    k.barrier()
    es.close()
    return nc


def _rope_tables(s):
    pos = np.arange(s * SEQ_C, (s + 1) * SEQ_C)
    row = (pos // 64).astype(np.float32)
    col = (pos % 64).astype(np.float32)
    inv = (1.0 / (np.float32(10000.0) ** (np.arange(0, 16, 2, dtype=np.float32) / np.float32(16)))).astype(np.float32)
    ang_r = row[:, None] * inv
    ang_c = col[:, None] * inv
    ang = np.concatenate([ang_r, ang_r, ang_c, ang_c], axis=-1).astype(np.float32)
    cos = np.cos(ang).astype(np.float32).T
    sin = np.sin(ang).astype(np.float32).T
    sign = np.ones((32, 1), np.float32)
    sign[0:8] = -1.0
    sign[16:24] = -1.0
    t = np.zeros((2, QK, NTOK), np.float32)
    t[0, :, :] = 1.0
    t[0, 64:96, 0:SEQ_C] = cos
    t[1, 64:96, 0:SEQ_C] = sin * sign
    return t


def _perm():
    p = np.arange(QK)
    for base in (64, 80):
        for i in range(8):
            p[base + i] = base + 8 + i
            p[base + 8 + i] = base + i
    return p


def _prep_shared(inp):
    f = lambda a: np.ascontiguousarray(np.asarray(a, dtype=np.float32))
    L = DEPTH
    perm = _perm()
    sh = {}
    sh["w_ada"] = f(inp["w_ada"])
    sh["b_ada_t"] = f(np.asarray(inp["b_ada"]).reshape(L, 72, 128).transpose(0, 2, 1))
    for n in ("w_ffn1_in", "w_ffn2_in", "w_ffn1_out", "w_ffn2_out", "w_mix_in", "w_mix_out"):
        sh[n] = f(inp[n])
    wq = np.asarray(inp["w_q_up"]).reshape(L, 384, NH, QK)
    wqp = wq[:, :, :, perm].copy()
    wqp[:, :, :, 0:64] = 0.0
    sh["w_qu"] = f(np.concatenate([wq.reshape(L, 384, NH * QK), wqp.reshape(L, 384, NH * QK)], axis=-1))
    wkv = np.asarray(inp["w_kv_up"]).reshape(L, 256, NH, 128)
    wkk = np.zeros((L, 256, NH, QK), np.float32)
    wkk[..., 0:64] = wkv[..., 0:64]
    sh["w_kvk"] = f(wkk.reshape(L, 256, NH * QK))
    sh["w_kvv"] = f(wkv[..., 64:128].reshape(L, 256, NH * 64))
    sh["w_spT"] = f(np.asarray(inp["w_spatial"]).transpose(0, 1, 3, 2))
    sh["b_sp"] = f(np.asarray(inp["b_spatial"]).reshape(L, 1, 512))
    vec = np.zeros((L, 128, NVEC), np.float32)
    colmaj = lambda v, n: np.asarray(v).reshape(L, n, 128).transpose(0, 2, 1)
    vec[:, :, V_GF1:V_GF1 + 8] = colmaj(inp["g_ffn1"], 8)
    vec[:, :, V_GMIX:V_GMIX + 8] = colmaj(inp["g_mix"], 8)
    vec[:, :, V_GF2:V_GF2 + 8] = colmaj(inp["g_ffn2"], 8)
    vec[:, :, V_GQL:V_GQL + 3] = colmaj(inp["g_q_lat"], 3)
    vec[:, :, V_GKV:V_GKV + 2] = colmaj(inp["g_kv_lat"], 2)
    gq = np.asarray(inp["g_q_head"]); gk = np.asarray(inp["g_k_head"])
    vec[:, 0:QK, V_GQH] = gq
    vec[:, 0:QK, V_GQHP] = gq[:, perm]
    vec[:, 0:QK, V_GKH] = gk
    vec[:, 0:QK, V_GKHP] = gk[:, perm]
    go = np.asarray(inp["g_out"])
    vec[:, 0:64, V_GOA:V_GOA + 8] = go[:, 0:512].reshape(L, 8, 64).transpose(0, 2, 1)
    vec[:, 0:64, V_GOS:V_GOS + 4] = go[:, 512:768].reshape(L, 4, 64).transpose(0, 2, 1)
    vec[:, :, V_GOC:V_GOC + 2] = go[:, 768:1024].reshape(L, 2, 128).transpose(0, 2, 1)
    wc = np.asarray(inp["w_conv"])
    for c in range(2):
        for kk in range(3):
            vec[:, :, V_WCV + c * 3 + kk] = wc[:, kk, c * 128:(c + 1) * 128]
    sh["vecs"] = f(vec)
    sh["gsgu_b"] = f(np.broadcast_to(np.asarray(inp["g_sgu"]).reshape(L, 1, 256), (L, 128, 256)))
    sh["grow"] = f(np.concatenate([gq, gk], axis=-1).reshape(L, 1, 2 * QK))
    sh["ident"] = np.eye(128, dtype=np.float32)
    sel = np.zeros((32, 2 * QK), np.float32)
    for i in range(32):
        sel[i, 64 + i] = 1.0
    for m in range(64, QK):
        sel[perm[m] - 64, QK + m] = 1.0
    sh["sel"] = sel
    return sh


def make_in_maps(inp):
    sh = _prep_shared(inp)
    x = np.asarray(inp["x"], dtype=np.float32)
    c = np.asarray(inp["c"], dtype=np.float32)
    ctx = np.asarray(inp["ctx"], dtype=np.float32)
    c_ctx = np.asarray(inp["c_ctx"], dtype=np.float32)
    ropes = [_rope_tables(0), _rope_tables(1)]
    maps = []
    for i in range(8):
        b, s = i // 2, i % 2
        m = dict(sh)
        m["x"] = np.ascontiguousarray(x[b, s * SEQ_C:(s + 1) * SEQ_C, :])
        m["ctx"] = np.ascontiguousarray(ctx[b])
        ccv = np.stack([c[b].reshape(8, 128).T, c_ctx.reshape(8, 128).T], axis=-1)
        m["cc"] = np.ascontiguousarray(ccv.astype(np.float32))
        m["rope"] = ropes[s]
        hmk = np.zeros((128, 2), np.float32)
        hmk[:, 0] = 1.0 if s == 1 else 0.0
        hmk[:, 1] = 1.0 if s == 0 else 0.0
        m["hmask"] = hmk
        maps.append(m)
    return maps


_NC_CACHE = {}


def kernel(**inputs):
    if "nc" not in _NC_CACHE:
        _NC_CACHE["nc"] = build_program()
    nc = _NC_CACHE["nc"]
    in_maps = make_in_maps(inputs)
    res = run_bass_kernel_spmd(nc, in_maps, core_ids=list(range(8)))
    outp = np.empty((4, 2 * SEQ_C, D), np.float32)
    for i in range(8):
        b, s = i // 2, i % 2
        outp[b, s * SEQ_C:(s + 1) * SEQ_C, :] = res.results[i]["out"]
    return outp
```
